# Optimizing a Trainium2 kernel written in Bass

```python
import math
import jax, jax.numpy as jnp
from jax import lax
import numpy as np

D_MODEL = 4096
BATCH = 8
SEQ = 2048
DEPTH = 2

GRID_W = 64
CTX_LEN = 256
N_MIXERS = 2
HEAD_DIM = 128
DIFF_HEADS = D_MODEL // (2 * HEAD_DIM)
DIFF_V_DIM = 2 * HEAD_DIM
GQA_HEADS = D_MODEL // HEAD_DIM
GQA_KV_HEADS = GQA_HEADS // 4
GQA_GROUP = GQA_HEADS // GQA_KV_HEADS
GQA_KV_DIM = GQA_KV_HEADS * HEAD_DIM
N_EXPERTS = 16
EC_CAPACITY_FACTOR = 2
EXPERT_FF = D_MODEL // 4
ROPE_THETA = 10000.0
ROPE_AXIS_DIM = HEAD_DIM // 2
Q_BLOCK = 128
NORM_EPS = 1e-6
N_MOD = 6

kernel_name = "hybrid_diffattn_gqa_ecmoe_dit"


def rms_norm(x, g):
    xf = x.astype(jnp.float32)
    y = xf * lax.rsqrt(jnp.mean(xf * xf, axis=-1, keepdims=True) + NORM_EPS)
    return (y * g.astype(jnp.float32)).astype(x.dtype)


def adaln(cond, w, b):
    m = jax.nn.silu(cond) @ w + b
    return m.reshape(cond.shape[0], N_MOD, D_MODEL)


def modulate(h, shift, scale):
    return h * (1 + scale) + shift


def axial_rope(n):
    rows = n // GRID_W
    row = jnp.repeat(jnp.arange(rows, dtype=jnp.float32), GRID_W)
    col = jnp.tile(jnp.arange(GRID_W, dtype=jnp.float32), rows)
    inv_freq = ROPE_THETA ** (-jnp.arange(0, ROPE_AXIS_DIM, 2, dtype=jnp.float32) / ROPE_AXIS_DIM)
    ang = jnp.concatenate([row[:, None] * inv_freq, col[:, None] * inv_freq], axis=-1)
    return jnp.cos(ang), jnp.sin(ang)


def apply_rope(x, cos, sin):
    shape = (cos.shape[0],) + (1,) * (x.ndim - 3) + (cos.shape[1],)
    cos = cos.reshape(shape)
    sin = sin.reshape(shape)
    xf = x.astype(jnp.float32)
    half = HEAD_DIM // 2
    x1, x2 = xf[..., :half], xf[..., half:]
    out = jnp.concatenate([x1 * cos - x2 * sin, x2 * cos + x1 * sin], axis=-1)
    return out.astype(x.dtype)


def sweep_query_blocks(core, q):
    B, n = q.shape[:2]
    nb = n // Q_BLOCK
    qb = jnp.moveaxis(q.reshape((B, nb, Q_BLOCK) + q.shape[2:]), 1, 0)
    ob = lax.map(core, qb)
    return jnp.moveaxis(ob, 0, 1).reshape((B, n) + ob.shape[3:])


def diff_core(q, k, v, lam):
    s = jnp.einsum('bqhtd,bkhtd->bhtqk', q, k, preferred_element_type=jnp.float32) / math.sqrt(HEAD_DIM)
    p = jax.nn.softmax(s, axis=-1)
    a = p[:, :, 0] - lam * p[:, :, 1]
    return jnp.einsum('bhqk,bkhe->bqhe', a.astype(v.dtype), v)


def diff_attention(h_lat, h_ctx, wqkv, wo, qn, kn, lq1, lk1, lq2, lk2, subln, lam_init, cos, sin, need_ctx):
    def project(h):
        B, n, _ = h.shape
        q, k, v = jnp.split(h @ wqkv, 3, axis=-1)
        q = rms_norm(q.reshape(B, n, DIFF_HEADS, 2, HEAD_DIM), qn)
        k = rms_norm(k.reshape(B, n, DIFF_HEADS, 2, HEAD_DIM), kn)
        return q, k, v.reshape(B, n, DIFF_HEADS, DIFF_V_DIM)

    q_l, k_l, v_l = project(h_lat)
    q_l, k_l = apply_rope(q_l, cos, sin), apply_rope(k_l, cos, sin)
    q_c, k_c, v_c = project(h_ctx)
    lam = (jnp.exp(jnp.sum(lq1.astype(jnp.float32) * lk1.astype(jnp.float32)))
           - jnp.exp(jnp.sum(lq2.astype(jnp.float32) * lk2.astype(jnp.float32))) + lam_init)
    k_all = jnp.concatenate([k_l, k_c], axis=1)
    v_all = jnp.concatenate([v_l, v_c], axis=1)

    def finish(o):
        B, n = o.shape[:2]
        o = rms_norm(o, subln) * (1 - lam_init)
        return o.reshape(B, n, D_MODEL) @ wo

    out_lat = finish(sweep_query_blocks(lambda qb: diff_core(qb, k_all, v_all, lam), q_l))
    out_ctx = finish(diff_core(q_c, k_c, v_c, lam)) if need_ctx else None
    return out_lat, out_ctx


def gqa_core(q, k, v):
    s = jnp.einsum('bqhgd,bkhd->bhgqk', q, k, preferred_element_type=jnp.float32) / math.sqrt(HEAD_DIM)
    p = jax.nn.softmax(s, axis=-1)
    return jnp.einsum('bhgqk,bkhd->bqhgd', p.astype(v.dtype), v)


def gqa_attention(h_lat, h_ctx, wqkv, wo, qn, kn, cos, sin, need_ctx):
    def project(h):
        B, n, _ = h.shape
        qkv = h @ wqkv
        q = rms_norm(qkv[..., :D_MODEL].reshape(B, n, GQA_KV_HEADS, GQA_GROUP, HEAD_DIM), qn)
        k = rms_norm(qkv[..., D_MODEL:D_MODEL + GQA_KV_DIM].reshape(B, n, GQA_KV_HEADS, HEAD_DIM), kn)
        v = qkv[..., D_MODEL + GQA_KV_DIM:].reshape(B, n, GQA_KV_HEADS, HEAD_DIM)
        return q, k, v

    q_l, k_l, v_l = project(h_lat)
    q_l, k_l = apply_rope(q_l, cos, sin), apply_rope(k_l, cos, sin)
    q_c, k_c, v_c = project(h_ctx)
    k_all = jnp.concatenate([k_l, k_c], axis=1)
    v_all = jnp.concatenate([v_l, v_c], axis=1)

    def finish(o):
        B, n = o.shape[:2]
        return o.reshape(B, n, D_MODEL) @ wo

    out_lat = finish(sweep_query_blocks(lambda qb: gqa_core(qb, k_all, v_all), q_l))
    out_ctx = finish(gqa_core(q_c, k_c, v_c)) if need_ctx else None
    return out_lat, out_ctx


def ec_moe(h, router, wg, wu, wd):
    B, n, D = h.shape
    cap = EC_CAPACITY_FACTOR * n // N_EXPERTS
    aff = jax.nn.softmax((h @ router).astype(jnp.float32), axis=-1)
    gates, idx = lax.top_k(jnp.swapaxes(aff, 1, 2), cap)
    xs = jax.vmap(lambda hb, ib: hb[ib])(h, idx)
    hid = jax.nn.silu(jnp.einsum('becd,edf->becf', xs, wg)) * jnp.einsum('becd,edf->becf', xs, wu)
    y = jnp.einsum('becf,efd->becd', hid, wd) * gates[..., None].astype(h.dtype)
    return jax.vmap(lambda yb, ib: jnp.zeros((n, D), h.dtype).at[ib.reshape(-1)].add(yb.reshape(-1, D)))(y, idx)


def setup_inputs(seed: int = 0) -> dict:
    key = jax.random.key(seed)
    ks = iter(jax.random.split(key, 48))
    f32 = jnp.float32

    def nrm(shape, scale):
        return jax.random.normal(next(ks), shape, f32) * scale

    def gain(n):
        return 1.0 + nrm((n,), 0.02)

    D = D_MODEL
    inp = {}
    inp["x"] = nrm((BATCH, SEQ, D), 1.0)
    inp["c"] = nrm((BATCH, D), 1.0)
    inp["ctx"] = nrm((BATCH, CTX_LEN, D), 1.0)
    inp["c_ctx"] = nrm((D,), 1.0)
    inp["mod_w_0"] = nrm((D, N_MOD * D), 0.2 * D ** -0.5)
    inp["mod_b_0"] = nrm((N_MOD * D,), 0.02)
    inp["norm1_0"] = gain(D)
    inp["norm2_0"] = gain(D)
    inp["wqkv_0"] = nrm((D, 3 * D), D ** -0.5)
    inp["wo_0"] = nrm((D, D), D ** -0.5)
    inp["qnorm_0"] = gain(HEAD_DIM)
    inp["knorm_0"] = gain(HEAD_DIM)
    inp["lam_q1_0"] = nrm((HEAD_DIM,), 0.1)
    inp["lam_k1_0"] = nrm((HEAD_DIM,), 0.1)
    inp["lam_q2_0"] = nrm((HEAD_DIM,), 0.1)
    inp["lam_k2_0"] = nrm((HEAD_DIM,), 0.1)
    inp["subln_0"] = gain(DIFF_V_DIM)
    inp["router_0"] = nrm((D, N_EXPERTS), D ** -0.5)
    inp["we_gate_0"] = nrm((N_EXPERTS, D, EXPERT_FF), D ** -0.5)
    inp["we_up_0"] = nrm((N_EXPERTS, D, EXPERT_FF), D ** -0.5)
    inp["we_down_0"] = nrm((N_EXPERTS, EXPERT_FF, D), EXPERT_FF ** -0.5)
    inp["mod_w_1"] = nrm((D, N_MOD * D), 0.2 * D ** -0.5)
    inp["mod_b_1"] = nrm((N_MOD * D,), 0.02)
    inp["norm1_1"] = gain(D)
    inp["norm2_1"] = gain(D)
    inp["wqkv_1"] = nrm((D, D + 2 * GQA_KV_DIM), D ** -0.5)
    inp["wo_1"] = nrm((D, D), D ** -0.5)
    inp["qnorm_1"] = gain(HEAD_DIM)
    inp["knorm_1"] = gain(HEAD_DIM)
    inp["router_1"] = nrm((D, N_EXPERTS), D ** -0.5)
    inp["we_gate_1"] = nrm((N_EXPERTS, D, EXPERT_FF), D ** -0.5)
    inp["we_up_1"] = nrm((N_EXPERTS, D, EXPERT_FF), D ** -0.5)
    inp["we_down_1"] = nrm((N_EXPERTS, EXPERT_FF, D), EXPERT_FF ** -0.5)
    return inp


def reference(x, c, ctx, c_ctx,
              mod_w_0, mod_b_0, norm1_0, norm2_0, wqkv_0, wo_0, qnorm_0, knorm_0,
              lam_q1_0, lam_k1_0, lam_q2_0, lam_k2_0, subln_0, router_0, we_gate_0, we_up_0, we_down_0,
              mod_w_1, mod_b_1, norm1_1, norm2_1, wqkv_1, wo_1, qnorm_1, knorm_1,
              router_1, we_gate_1, we_up_1, we_down_1):
    layers = [
        dict(mod_w=mod_w_0, mod_b=mod_b_0, norm1=norm1_0, norm2=norm2_0, wqkv=wqkv_0, wo=wo_0,
             qn=qnorm_0, kn=knorm_0, lq1=lam_q1_0, lk1=lam_k1_0, lq2=lam_q2_0, lk2=lam_k2_0,
             subln=subln_0, router=router_0, wg=we_gate_0, wu=we_up_0, wd=we_down_0),
        dict(mod_w=mod_w_1, mod_b=mod_b_1, norm1=norm1_1, norm2=norm2_1, wqkv=wqkv_1, wo=wo_1,
             qn=qnorm_1, kn=knorm_1, router=router_1, wg=we_gate_1, wu=we_up_1, wd=we_down_1),
    ]
    cos, sin = axial_rope(x.shape[1])
    x_lat, x_ctx = x, ctx
    for i in range(DEPTH):
        p = layers[i]
        need_ctx = i < DEPTH - 1
        m_lat = adaln(c, p["mod_w"], p["mod_b"])
        m_ctx = adaln(c_ctx[None], p["mod_w"], p["mod_b"])
        sh1, sc1, g1, sh2, sc2, g2 = [m_lat[:, j, None, :] for j in range(N_MOD)]
        csh1, csc1, cg1, csh2, csc2, cg2 = [m_ctx[:, j, None, :] for j in range(N_MOD)]

        h_lat = modulate(rms_norm(x_lat, p["norm1"]), sh1, sc1)
        h_ctx = modulate(rms_norm(x_ctx, p["norm1"]), csh1, csc1)
        if i % N_MIXERS == 0:
            lam_init = 0.8 - 0.6 * math.exp(-0.3 * i)
            o_lat, o_ctx = diff_attention(h_lat, h_ctx, p["wqkv"], p["wo"], p["qn"], p["kn"],
                                          p["lq1"], p["lk1"], p["lq2"], p["lk2"], p["subln"],
                                          lam_init, cos, sin, need_ctx)
        else:
            o_lat, o_ctx = gqa_attention(h_lat, h_ctx, p["wqkv"], p["wo"], p["qn"], p["kn"],
                                         cos, sin, need_ctx)
        x_lat = x_lat + g1 * o_lat

        h2_lat = modulate(rms_norm(x_lat, p["norm2"]), sh2, sc2)
        x_lat = x_lat + g2 * ec_moe(h2_lat, p["router"], p["wg"], p["wu"], p["wd"])
        if need_ctx:
            x_ctx = x_ctx + cg1 * o_ctx
            h2_ctx = modulate(rms_norm(x_ctx, p["norm2"]), csh2, csc2)
            x_ctx = x_ctx + cg2 * ec_moe(h2_ctx, p["router"], p["wg"], p["wu"], p["wd"])
    return x_lat
```

```python
import math
from contextlib import ExitStack

import numpy as np
import concourse.bass as bass
import concourse.mybir as mybir
from concourse.bass_utils import run_bass_kernel_spmd

F32 = mybir.dt.float32
BF16 = mybir.dt.bfloat16
U32 = mybir.dt.uint32
AF = mybir.ActivationFunctionType
ALU = mybir.AluOpType
AX = mybir.AxisListType

NORM_EPS = 1e-6
N_EXPERTS = 16


class Cfg:
    def __init__(s, D=4096, T=2048, C=256, FF=1024, GW=64):
        s.D, s.T, s.C, s.FF, s.GW = D, T, C, FF, GW
        s.KC = D // 128
        s.TT = T + C
        s.NT = s.TT // 128
        s.NTL = T // 128
        s.NTC = C // 128
        s.H0 = D // 256
        s.H1 = D // 128
        s.KV1 = s.H1 // 4
        s.KVD = s.KV1 * 128
        s.E = N_EXPERTS
        s.CAPL = 2 * T // s.E
        s.CAPC = 2 * C // s.E
        s.FC = FF // 128


class Tile:
    def __init__(s, h):
        s.h = h
        s.w = None
        s.r = []

    def __getitem__(s, k):
        return s.h[k]


class Res(Tile):
    def __init__(s):
        s.h = None
        s.w = None
        s.r = []


class KB:
    NQ = 8

    def __init__(s, nc):
        s.nc = nc
        s.E = {"pe": nc.tensor, "act": nc.scalar, "dve": nc.vector, "pool": nc.gpsimd, "sp": nc.sync}
        s.csem = {e: nc.alloc_semaphore(name=f"cs_{e}") for e in ("pe", "act", "dve", "pool")}
        s.cnt = dict.fromkeys(s.csem, 0)
        s.waited = {e: {} for e in s.E}
        s.dq = {}
        for q in ("sp", "pool"):
            s.dq[q] = {"sems": [nc.alloc_semaphore(name=f"dq_{q}{i}") for i in range(s.NQ)],
                       "vals": [0] * s.NQ, "i": 0}
        s.pending = []
        s.uid = 0

    def sb(s, st, name, shape, dt):
        s.uid += 1
        return Tile(st.enter_context(s.nc.sbuf_tensor(f"{name}_{s.uid}", list(shape), dt)))

    def ps(s, st, name, shape, dt=F32):
        s.uid += 1
        return Tile(st.enter_context(s.nc.psum_tensor(f"{name}_{s.uid}", list(shape), dt)))

    def _wait(s, e, tok):
        if tok is None:
            return
        sem, val, owner = tok
        if owner == e:
            if e == "pe":
                return
            if val > s.cnt[e]:
                return
        key = sem.name
        if s.waited[e].get(key, 0) >= val:
            return
        s.E[e].wait_ge(sem, val)
        s.waited[e][key] = val

    def _deps(s, e, reads, writes):
        for r in reads:
            s._wait(e, r.w)
        for w in writes:
            s._wait(e, w.w)
            for t in w.r:
                s._wait(e, t)

    def _commit(s, tok, reads, writes):
        for r in reads:
            r.r.append(tok)
            if len(r.r) > 24:
                r.r = r.r[-24:]
        for w in writes:
            w.w = tok
            w.r = []

    def op(s, e, fn, reads=(), writes=(), inc=True):
        s._deps(e, reads, writes)
        ins = fn()
        if inc:
            s.cnt[e] += 1
            ins.then_inc(s.csem[e], 1)
            tok = (s.csem[e], s.cnt[e], e)
        else:
            tok = (s.csem[e], s.cnt[e] + 1, e)
        s._commit(tok, reads, writes)
        return ins

    def dma(s, q, fns, reads=(), writes=()):
        d = s.dq[q]
        i = d["i"]
        sem = d["sems"][i]
        if d["vals"][i] > 0:
            s._wait(q, (sem, d["vals"][i], "dma"))
        s._deps(q, reads, writes)
        for fn in fns:
            fn(s.E[q]).then_inc(sem, 16)
            d["vals"][i] += 16
        tok = (sem, d["vals"][i], "dma")
        d["i"] = (i + 1) % s.NQ
        s._commit(tok, reads, writes)
        s.pending.append(tok)
        return tok

    def barrier(s):
        for e in s.E:
            for f in s.csem:
                if f != e and s.cnt[f] > 0:
                    s._wait(e, (s.csem[f], s.cnt[f], f))
            for tok in s.pending:
                s._wait(e, tok)
        s.pending = []


def split_groups(n, mx):
    g = (n + mx - 1) // mx
    base, rem = divmod(n, g)
    out, a = [], 0
    for i in range(g):
        k = base + (1 if i < rem else 0)
        out.append((a, k))
        a += k
    return out


class Prog:
    def __init__(s, cfg, dbg=False, stop_after=None, moe_stop=None):
        s.moe_stop = moe_stop
        s.cfg = cfg
        s.dbg = dbg
        s.stop_after = stop_after
        nc = s.nc = bass.Bass("TRN2", target_bir_lowering=False)
        s.kb = KB(nc)
        D, T, C, TT, FF, E = cfg.D, cfg.T, cfg.C, cfg.TT, cfg.FF, cfg.E

        def inp(name, shape):
            return nc.dram_tensor(name, list(shape), F32, kind="ExternalInput").ap()

        s.x = inp("x", [T, D])
        s.c = inp("c", [1, D])
        s.ctx = inp("ctx", [C, D])
        s.c_ctx = inp("c_ctx", [1, D])
        s.L = []
        for l in range(2):
            p = {}
            p["mod_w"] = inp(f"mod_w_{l}", [D, 6 * D])
            p["mod_b"] = inp(f"mod_b_{l}", [1, 6 * D])
            p["norm1"] = inp(f"norm1_{l}", [1, D])
            p["norm2"] = inp(f"norm2_{l}", [1, D])
            p["wqkv"] = inp(f"wqkv_{l}", [D, 3 * D if l == 0 else D + 2 * cfg.KVD])
            p["wo"] = inp(f"wo_{l}", [D, D])
            p["qn"] = inp(f"qnorm_{l}", [1, 128])
            p["kn"] = inp(f"knorm_{l}", [1, 128])
            if l == 0:
                p["lam_q1"] = inp("lam_q1_0", [1, 128])
                p["lam_k1"] = inp("lam_k1_0", [1, 128])
                p["lam_q2"] = inp("lam_q2_0", [1, 128])
                p["lam_k2"] = inp("lam_k2_0", [1, 128])
                p["subln"] = inp(f"subln_{l}", [1, 256])
            p["router"] = inp(f"router_{l}", [D, E])
            p["wg"] = inp(f"we_gate_{l}", [E, D, FF])
            p["wu"] = inp(f"we_up_{l}", [E, D, FF])
            p["wd"] = inp(f"we_down_{l}", [E, FF, D])
            s.L.append(p)
        s.k_rope_c = inp("k_rope_c", [128, TT])
        s.k_rope_s = inp("k_rope_s", [128, TT])
        s.k_ident = inp("k_ident", [128, 128])
        s.k_rot = inp("k_rot", [128, 128])

        s.out = nc.dram_tensor("out", [T, D], F32, kind="ExternalOutput").ap()

        s.scr = {}

        def scratch(name, shape, dt):
            kind = "ExternalOutput" if dbg else "Internal"
            s.scr[name] = nc.dram_tensor("s_" + name, list(shape), dt, kind=kind).ap()
            return s.scr[name]

        s.modv = scratch("modv", [4, 6 * D], F32)
        s.xres = scratch("xres", [TT, D], F32)
        s.hT = scratch("hT", [128, cfg.KC, TT], BF16)
        s.oT = scratch("oT", [128, cfg.KC, TT], BF16)
        s.qkT = scratch("qkT", [2 * cfg.KC, 128, TT], BF16)
        s.v = scratch("v", [TT, D], BF16)
        s.h2 = scratch("h2", [TT, D], BF16)
        if dbg:
            s.d_aff = scratch("d_aff", [cfg.E, T], F32)
            s.d_idx = scratch("d_idx", [128, D // 512 + 1, cfg.E], U32)
            s.d_gate = scratch("d_gate", [128, cfg.E], F32)

        s.build()

    def build(s):
        kb, nc, cfg = s.kb, s.nc, s.cfg
        with ExitStack() as gst:
            s.consts(gst)
            NBm = 6 * cfg.D // 512
            nfirst = 2 * cfg.D // 512
            for _ in s.adaln_gen([(0, i) for i in range(nfirst)]):
                pass
            kb.barrier()
            bgwork = [(0, i) for i in range(nfirst, NBm)] + [(1, i) for i in range(NBm)]
            phases = []
            for l in range(2):
                phases.append((f"norm1_{l}", lambda l=l: s.phase_norm1(l)))
                phases.append((f"qkv_{l}", lambda l=l: s.phase_qkv(l)))
                phases.append((f"attn_{l}", lambda l=l: s.phase_attn(l)))
                phases.append((f"wo_{l}", lambda l=l: s.phase_wo(l)))
                phases.append((f"moe_{l}", lambda l=l: s.phase_moe(l)))
            for name, fn in phases:
                if name == "attn_0":
                    gen = s.adaln_gen(bgwork, shared_psum=True)
                    state = {"done": False}

                    def step(gen=gen, state=state):
                        if not state["done"]:
                            if next(gen) == "done":
                                state["done"] = True

                    def bg(state=state):
                        state["n"] = state.get("n", 0) + 1
                        if state["n"] % 2 == 0:
                            step()
                    step()
                    s.phase_attn(0, bg=bg)
                    while not state["done"]:
                        step()
                    for _ in gen:
                        pass
                else:
                    fn()
                kb.barrier()
                if s.stop_after == name:
                    break

    def consts(s, st):
        kb, nc = s.kb, s.nc
        s.ident_f = kb.sb(st, "ident_f", [128, 128], F32)
        s.ident_b = kb.sb(st, "ident_b", [128, 128], BF16)
        s.rot_b = kb.sb(st, "rot_b", [128, 128], BF16)
        s.ones_b = kb.sb(st, "ones_b", [128, 128], BF16)
        tmp = kb.sb(st, "ctmp", [128, 128], F32)
        kb.dma("sp", [lambda e: e.dma_start(out=s.ident_f[:, :], in_=s.k_ident[:, :])], writes=[s.ident_f])
        kb.dma("sp", [lambda e: e.dma_start(out=tmp[:, :], in_=s.k_rot[:, :])], writes=[tmp])
        kb.op("dve", lambda: nc.vector.tensor_copy(out=s.ident_b[:, :], in_=s.ident_f[:, :]),
              reads=[s.ident_f], writes=[s.ident_b])
        kb.op("dve", lambda: nc.vector.tensor_copy(out=s.rot_b[:, :], in_=tmp[:, :]),
              reads=[tmp], writes=[s.rot_b])
        kb.op("dve", lambda: nc.vector.memset(s.ones_b[:, :], 1.0), writes=[s.ones_b])
        s.eps_col = kb.sb(st, "eps_col", [128, 1], F32)
        kb.op("dve", lambda: nc.vector.memset(s.eps_col[:, :], NORM_EPS), writes=[s.eps_col])

    def load_w_block(s, q, tile, w_ap, c0, ncols, nk=None):
        kb = s.kb
        src = w_ap[:, c0:c0 + ncols].rearrange("(k p) n -> p k n", p=128)
        KCn = src.shape[1]
        step = max(1, 1024 // 128)
        fns = []
        for k0 in range(0, KCn, step):
            k1 = min(KCn, k0 + step)
            fns.append(lambda e, k0=k0, k1=k1: e.dma_start(out=tile[:, k0:k1, 0:ncols], in_=src[:, k0:k1, :]))
        kb.dma(q, fns, writes=[tile])

    def adaln_gen(s, work, shared_psum=False):
        kb, nc, cfg = s.kb, s.nc, s.cfg
        D, KC = cfg.D, cfg.KC
        with ExitStack() as st:
            craw = kb.sb(st, "craw", [KC, 2, 128], F32)
            csil = kb.sb(st, "csil", [KC, 2, 128], F32)
            sT = kb.sb(st, "sT", [128, KC, 2], BF16)
            pacc = [kb.ps(st, "macc", [128, 512], F32) for _ in range(1 if shared_psum else 2)]
            ps_t = pacc[0]
            kb.dma("sp", [lambda e: e.dma_start(out=craw[:, 0, :], in_=s.c.rearrange("o (k p) -> (o k) p", p=128))],
                   writes=[craw])
            kb.dma("sp", [lambda e: e.dma_start(out=craw[:, 1, :], in_=s.c_ctx.rearrange("o (k p) -> (o k) p", p=128))],
                   writes=[craw])
            kb.op("act", lambda: nc.scalar.activation(out=csil[:, :, :], in_=craw[:, :, :], func=AF.Silu),
                  reads=[craw], writes=[csil])
            for r in range(2):
                kb.op("pe", lambda r=r: nc.tensor.transpose(out=ps_t[:, r * KC:(r + 1) * KC], in_=csil[:, r, :],
                                                              identity=s.ident_f[0:KC, 0:KC]),
                      reads=[csil, s.ident_f], writes=[ps_t])
            for r in range(2):
                kb.op("dve", lambda r=r: nc.vector.tensor_copy(out=sT[:, :, r], in_=ps_t[:, r * KC:(r + 1) * KC]),
                      reads=[ps_t], writes=[sT])
            NBUF = 3
            wb = [kb.sb(st, "mw", [128, KC, 512], BF16) for _ in range(NBUF)]
            bb = [kb.sb(st, "mb", [2, 512], F32) for _ in range(NBUF)]
            ob = [kb.sb(st, "mo", [2, 512], F32) for _ in range(NBUF)]

            def load(n):
                l, i = work[n]
                p = s.L[l]
                s.load_w_block("pool", wb[n % NBUF], p["mod_w"], i * 512, 512)
                kb.dma("pool", [lambda e, r=r: e.dma_start(out=bb[n % NBUF][r:r + 1, :],
                                                           in_=p["mod_b"][0:1, i * 512:(i + 1) * 512])
                              for r in range(2)], writes=[bb[n % NBUF]])

            if work:
                load(0)
            for n in range(len(work)):
                l, i = work[n]
                if n + 1 < len(work):
                    load(n + 1)
                w, acc, o, b = wb[n % NBUF], pacc[n % len(pacc)], ob[n % NBUF], bb[n % NBUF]
                for kc in range(KC):
                    kb.op("pe", lambda kc=kc: nc.tensor.matmul(acc[0:2, :], lhsT=sT[:, kc, :], rhs=w[:, kc, :],
                                                               start=(kc == 0), stop=(kc == KC - 1)),
                          reads=[sT, w], writes=[acc], inc=(kc == KC - 1))
                kb.op("dve", lambda: nc.vector.tensor_tensor(out=o[:, :], in0=acc[0:2, :], in1=b[:, :], op=ALU.add),
                      reads=[acc, b], writes=[o])
                kb.dma("pool", [lambda e: e.dma_start(out=s.modv[2 * l:2 * l + 2, i * 512:(i + 1) * 512], in_=o[:, :])],
                       reads=[o])
                yield n
            yield "done"

    def bc_load(s, tile, src_row_ap):
        s.kb.dma("sp", [lambda e: e.dma_start(out=tile[:, :], in_=src_row_ap.partition_broadcast(128))],
                 writes=[tile])

    def mod_row(s, l, r, j):
        D = s.cfg.D
        return s.modv[2 * l + r:2 * l + r + 1, j * D:(j + 1) * D]

    def x_src(s, l, which, ti):
        cfg = s.cfg
        lat = ti < cfg.NTL
        if l == 0 and which == "in":
            return s.x[ti * 128:(ti + 1) * 128, :] if lat else s.ctx[(ti - cfg.NTL) * 128:(ti - cfg.NTL + 1) * 128, :]
        if l == 1 and which == "mid":
            assert lat
            return s.out[ti * 128:(ti + 1) * 128, :]
        return s.xres[ti * 128:(ti + 1) * 128, :]

    def x_dst(s, l, ti):
        if l == 1:
            return s.out[ti * 128:(ti + 1) * 128, :]
        return s.xres[ti * 128:(ti + 1) * 128, :]

    def norm_setup(s, st, l, which, stg=None):
        kb, nc, cfg = s.kb, s.nc, s.cfg
        D = cfg.D
        p = s.L[l]
        jsh, jsc = (0, 1) if which == 1 else (3, 4)
        AB = []
        nr = 2 if (l == 0 or which == 1) else 1
        for r in range(nr):
            AB.append((kb.sb(st, "nA", [128, D], F32), kb.sb(st, "nB", [128, D], F32)))
        gain = kb.sb(stg if stg is not None else st, "gain", [128, D], F32)
        s.bc_load(gain, p["norm1" if which == 1 else "norm2"][0:1, :])
        for r, (A, B) in enumerate(AB):
            s.bc_load(A, s.mod_row(l, r, jsc))
            s.bc_load(B, s.mod_row(l, r, jsh))
            kb.op("dve", lambda A=A: nc.vector.scalar_tensor_tensor(out=A[:, :], in0=A[:, :], scalar=1.0, in1=gain[:, :],
                                                                    op0=ALU.add, op1=ALU.mult),
                  reads=[gain, A], writes=[A])
        return AB

    def rsqrt_col(s, out, in_, np_, scale, eps):
        kb, nc = s.kb, s.nc
        kb.op("act", lambda: nc.scalar.activation(out=out[0:np_, 0:1], in_=in_[0:np_, 0:1], func=AF.Sqrt,
                                                  scale=scale, bias=s.eps_col[0:np_, 0:1] if eps == NORM_EPS else eps),
              reads=[in_, s.eps_col], writes=[out])
        kb.op("dve", lambda: nc.vector.reciprocal(out=out[0:np_, 0:1], in_=out[0:np_, 0:1]), reads=[out], writes=[out])

    def norm_tile(s, xt, A, B, ssq, rstd, junk, y_out, h_out):
        kb, nc, D = s.kb, s.nc, s.cfg.D
        kb.op("act", lambda: nc.scalar.activation(out=junk[:, :], in_=xt[:, :], func=AF.Square, accum_out=ssq[:, 0:1]),
              reads=[xt], writes=[junk, ssq])
        s.rsqrt_col(rstd, ssq, 128, 1.0 / D, NORM_EPS)
        kb.op("dve", lambda: nc.vector.scalar_tensor_tensor(out=y_out[:, :], in0=xt[:, :], scalar=rstd[:, 0:1], in1=A[:, :],
                                                            op0=ALU.mult, op1=ALU.mult),
              reads=[xt, rstd, A], writes=[y_out])
        kb.op("dve", lambda: nc.vector.tensor_tensor(out=h_out[:, :], in0=y_out[:, :], in1=B[:, :], op=ALU.add),
              reads=[y_out, B], writes=[h_out])

    def phase_norm1(s, l):
        kb, nc, cfg = s.kb, s.nc, s.cfg
        D, KC = cfg.D, cfg.KC
        with ExitStack() as st:
            AB = s.norm_setup(st, l, 1)
            xt = [kb.sb(st, "xt", [128, D], F32) for _ in range(2)]
            hb = [kb.sb(st, "hb", [128, D], BF16) for _ in range(2)]
            junk = kb.sb(st, "junk", [128, D], BF16)
            ssq = kb.sb(st, "ssq", [128, 1], F32)
            rstd = kb.sb(st, "rstd", [128, 1], F32)
            hst = [kb.sb(st, "hst", [128, KC, 128], BF16) for _ in range(2)]
            NPB = min(8, KC)
            pst = [kb.ps(st, "pst", [128, 8, 128], BF16) for _ in range(4)]
            pi = [0]

            def stage1(ti):
                A, B = AB[0 if ti < cfg.NTL else 1]
                x_t, h_t = xt[ti % 2], hb[ti % 2]
                kb.dma("sp", [lambda e: e.dma_start(out=x_t[:, :], in_=s.x_src(l, "in", ti))], writes=[x_t])
                s.norm_tile(x_t, A, B, ssq, rstd, junk, x_t, h_t)

            def stage2(ti):
                h_t, stg = hb[ti % 2], hst[ti % 2]
                for k0 in range(0, KC, NPB):
                    pt = pst[pi[0] % 4]
                    pi[0] += 1
                    for j in range(NPB):
                        kb.op("pe", lambda j=j: nc.tensor.transpose(out=pt[:, j, :], in_=h_t[:, (k0 + j) * 128:(k0 + j + 1) * 128],
                                                                    identity=s.ident_b[:, :]),
                              reads=[h_t, s.ident_b], writes=[pt], inc=(j == NPB - 1))
                    kb.op("act", lambda: nc.scalar.copy(out=stg[:, k0:k0 + NPB, :], in_=pt[:, 0:NPB, :]),
                          reads=[pt], writes=[stg])
                step = min(8, KC)
                kb.dma("sp", [lambda e, k0=k0: e.dma_start(out=s.hT[:, k0:k0 + step, ti * 128:(ti + 1) * 128],
                                                            in_=stg[:, k0:k0 + step, :])
                              for k0 in range(0, KC, step)], reads=[stg])

            stage1(0)
            for ti in range(cfg.NT):
                if ti + 1 < cfg.NT:
                    stage1(ti + 1)
                stage2(ti)

    def col_load(s, tile, row_ap, n):
        s.kb.dma("sp", [lambda e: e.dma_start(out=tile[0:n, 0:1], in_=row_ap.rearrange("o p -> p o"))], writes=[tile])

    def phase_qkv(s, l):
        kb, nc, cfg = s.kb, s.nc, s.cfg
        D, KC, TT, NT = cfg.D, cfg.KC, cfg.TT, cfg.NT
        p = s.L[l]
        nq = D // 128
        nfm = (2 * D) // 128 if l == 0 else (D + cfg.KVD) // 128
        vc0 = 2 * D if l == 0 else D + cfg.KVD
        vcols = D if l == 0 else cfg.KVD
        vb = min(256, vcols)
        with ExitStack() as st:
            gq = kb.sb(st, "gq", [128, 1], F32)
            gk = kb.sb(st, "gk", [128, 1], F32)
            s.col_load(gq, p["qn"], 128)
            s.col_load(gk, p["kn"], 128)
            cosT = kb.sb(st, "cosT", [128, TT], F32)
            sinT = kb.sb(st, "sinT", [128, TT], F32)
            kb.dma("sp", [lambda e: e.dma_start(out=cosT[:, :], in_=s.k_rope_c[:, :])], writes=[cosT])
            kb.dma("sp", [lambda e: e.dma_start(out=sinT[:, :], in_=s.k_rope_s[:, :])], writes=[sinT])
            TB = split_groups(NT, 9)
            mxnt = max(k for _, k in TB)
            hT_sb = kb.sb(st, "hT_sb", [128, KC, mxnt * 128], BF16)
            wfm = [kb.sb(st, "wfm", [128, KC, 128], BF16) for _ in range(3)]
            wv = [kb.sb(st, "wv", [128, KC, vb], BF16) for _ in range(2)]
            qst = [kb.sb(st, "qst", [128, mxnt * 128], BF16) for _ in range(2)]
            sqb = [kb.sb(st, "sqb", [128, 512], BF16) for _ in range(2)]
            rs = [kb.sb(st, "rs", [128, 512], F32) for _ in range(2)]
            xb = [kb.sb(st, "xb", [128, 512], BF16) for _ in range(2)]
            t1 = [kb.sb(st, "t1", [128, 512], F32) for _ in range(2)]
            t2 = [kb.sb(st, "t2", [128, 512], F32) for _ in range(2)]
            vst = [kb.sb(st, "vst", [128, vb], BF16) for _ in range(3)]
            pacc = [kb.ps(st, "pacc", [128, 512], F32) for _ in range(3)]
            pss = [kb.ps(st, "pss", [128, 512], F32) for _ in range(2)]
            psr = [kb.ps(st, "psr", [128, 512], F32) for _ in range(2)]
            pv = pacc
            for (t0, nt) in TB:
                kstep = min(8, KC)
                for k0 in range(0, KC, kstep):
                    kb.dma("sp", [lambda e, k0=k0: e.dma_start(out=hT_sb[:, k0:k0 + kstep, 0:nt * 128],
                                                                in_=s.hT[:, k0:k0 + kstep, t0 * 128:(t0 + nt) * 128])],
                           writes=[hT_sb])
                groups = split_groups(nt, 4)
                items = [(j, gidx, a, k) for j in range(nfm) for gidx, (a, k) in enumerate(groups)]
                s.load_w_block("pool", wfm[0], p["wqkv"], 0, 128)

                def M(i):
                    j, gidx, a, k = items[i]
                    if gidx == 0 and j + 1 < nfm:
                        s.load_w_block("pool", wfm[(j + 1) % 3], p["wqkv"], (j + 1) * 128, 128)
                    W, acc, n, c0 = wfm[j % 3], pacc[i % 3], k * 128, a * 128
                    for kc in range(KC):
                        kb.op("pe", lambda kc=kc: nc.tensor.matmul(acc[:, 0:n], lhsT=W[:, kc, :], rhs=hT_sb[:, kc, c0:c0 + n],
                                                                   start=(kc == 0), stop=(kc == KC - 1)),
                              reads=[W, hT_sb], writes=[acc], inc=(kc == KC - 1))

                def E1(i):
                    j, gidx, a, k = items[i]
                    acc, n, sq_, ss = pacc[i % 3], k * 128, sqb[i % 2], pss[i % 2]
                    kb.op("act", lambda: nc.scalar.activation(out=sq_[:, 0:n], in_=acc[:, 0:n], func=AF.Square),
                          reads=[acc], writes=[sq_])
                    kb.op("pe", lambda: nc.tensor.matmul(ss[:, 0:n], lhsT=s.ones_b[:, :], rhs=sq_[:, 0:n], start=True, stop=True),
                          reads=[s.ones_b, sq_], writes=[ss])

                def E2(i):
                    j, gidx, a, k = items[i]
                    acc, n, ss, rs_, xb_, sr = pacc[i % 3], k * 128, pss[i % 2], rs[i % 2], xb[i % 2], psr[i % 2]
                    gain = gq if j < nq else gk
                    kb.op("act", lambda: nc.scalar.activation(out=rs_[:, 0:n], in_=ss[:, 0:n], func=AF.Sqrt,
                                                              scale=1.0 / 128, bias=s.eps_col[:, 0:1]),
                          reads=[ss, s.eps_col], writes=[rs_])
                    kb.op("dve", lambda: nc.vector.reciprocal(out=rs_[:, 0:n], in_=rs_[:, 0:n]), reads=[rs_], writes=[rs_])
                    kb.op("dve", lambda: nc.vector.scalar_tensor_tensor(out=xb_[:, 0:n], in0=acc[:, 0:n], scalar=gain[:, 0:1],
                                                                        in1=rs_[:, 0:n], op0=ALU.mult, op1=ALU.mult),
                          reads=[acc, gain, rs_], writes=[xb_])
                    kb.op("pe", lambda: nc.tensor.matmul(sr[:, 0:n], lhsT=s.rot_b[:, :], rhs=xb_[:, 0:n], start=True, stop=True),
                          reads=[s.rot_b, xb_], writes=[sr])

                def E3(i):
                    j, gidx, a, k = items[i]
                    n, c0, tok0 = k * 128, a * 128, (t0 + a) * 128
                    xb_, sr, t1_, t2_, qs = xb[i % 2], psr[i % 2], t1[i % 2], t2[i % 2], qst[j % 2]
                    kb.op("dve", lambda: nc.vector.tensor_tensor(out=t1_[:, 0:n], in0=xb_[:, 0:n], in1=cosT[:, tok0:tok0 + n],
                                                                 op=ALU.mult), reads=[xb_, cosT], writes=[t1_])
                    kb.op("dve", lambda: nc.vector.tensor_tensor(out=t2_[:, 0:n], in0=sr[:, 0:n], in1=sinT[:, tok0:tok0 + n],
                                                                 op=ALU.mult), reads=[sr, sinT], writes=[t2_])
                    kb.op("dve", lambda: nc.vector.tensor_tensor(out=qs[:, c0:c0 + n], in0=t1_[:, 0:n], in1=t2_[:, 0:n],
                                                                 op=ALU.add), reads=[t1_, t2_], writes=[qs])
                    if gidx == len(groups) - 1:
                        kb.dma("sp", [lambda e: e.dma_start(out=s.qkT[j, :, t0 * 128:(t0 + nt) * 128], in_=qs[:, 0:nt * 128])],
                               reads=[qs])

                ni = len(items)
                for i in range(ni + 3):
                    if i < ni:
                        M(i)
                    if 0 <= i - 1 < ni:
                        E1(i - 1)
                    if 0 <= i - 2 < ni:
                        E2(i - 2)
                    if 0 <= i - 3 < ni:
                        E3(i - 3)
                nvb = vcols // vb
                s.load_w_block("pool", wv[0], p["wqkv"], vc0, vb)
                vi = 0
                for jv in range(nvb):
                    if jv + 1 < nvb:
                        s.load_w_block("pool", wv[(jv + 1) % 2], p["wqkv"], vc0 + (jv + 1) * vb, vb)
                    W = wv[jv % 2]
                    for ti in range(nt):
                        acc = pv[vi % 2]
                        vs = vst[vi % 3]
                        vi += 1
                        for kc in range(KC):
                            kb.op("pe", lambda kc=kc: nc.tensor.matmul(acc[:, 0:vb], lhsT=hT_sb[:, kc, ti * 128:(ti + 1) * 128],
                                                                       rhs=W[:, kc, :], start=(kc == 0), stop=(kc == KC - 1)),
                                  reads=[W, hT_sb], writes=[acc], inc=(kc == KC - 1))
                        kb.op("act", lambda: nc.scalar.copy(out=vs[:, :], in_=acc[:, 0:vb]), reads=[acc], writes=[vs])
                        kb.dma("sp", [lambda e: e.dma_start(out=s.v[(t0 + ti) * 128:(t0 + ti + 1) * 128, jv * vb:(jv + 1) * vb],
                                                            in_=vs[:, :])], reads=[vs])

    def phase_attn(s, l, bg=None):
        kb, nc, cfg = s.kb, s.nc, s.cfg
        D, KC, TT, NT, NTL, NTC = cfg.D, cfg.KC, cfg.TT, cfg.NT, cfg.NTL, cfg.NTC
        p = s.L[l]
        nq = D // 128
        dv = 256 if l == 0 else 128
        scale = 1.0 / math.sqrt(128.0)
        lam_init = 0.8 - 0.6 * math.exp(-0.3 * l)
        with ExitStack() as st:
            gqb = kb.sb(st, "gqb", [128, 128], F32)
            gkb = kb.sb(st, "gkb", [128, 128], F32)
            s.bc_load(gqb, p["qn"][0:1, :])
            s.bc_load(gkb, p["kn"][0:1, :])
            mq = kb.sb(st, "mq", [128, 1], F32)
            mk = kb.sb(st, "mk", [128, 1], F32)
            negC = kb.sb(st, "negC", [128, 1], F32)
            for (gb, m) in ((gqb, mq), (gkb, mk)):
                kb.op("act", lambda gb=gb: nc.scalar.activation(out=gb[:, :], in_=gb[:, :], func=AF.Abs), reads=[gb], writes=[gb])
                kb.op("dve", lambda gb=gb, m=m: nc.vector.tensor_reduce(out=m[:, 0:1], in_=gb[:, :], axis=AX.X, op=ALU.max),
                      reads=[gb], writes=[m])
            kb.op("dve", lambda: nc.vector.scalar_tensor_tensor(out=negC[:, 0:1], in0=mq[:, 0:1], scalar=-math.sqrt(128.0) * scale * 1.0,
                                                                in1=mk[:, 0:1], op0=ALU.mult, op1=ALU.mult),
                  reads=[mq, mk], writes=[negC])
            kb.op("dve", lambda: nc.vector.tensor_scalar(out=negC[:, 0:1], in0=negC[:, 0:1], scalar1=math.sqrt(128.0), scalar2=None,
                                                         op0=ALU.mult), reads=[negC], writes=[negC])
            if l == 0:
                lt = [kb.sb(st, "lt", [128, 128], F32) for _ in range(4)]
                for i, nm in enumerate(("lam_q1", "lam_k1", "lam_q2", "lam_k2")):
                    s.bc_load(lt[i], p[nm][0:1, :])
                d1 = kb.sb(st, "d1", [128, 1], F32)
                d2 = kb.sb(st, "d2", [128, 1], F32)
                neglam = kb.sb(st, "neglam", [128, 1], F32)
                for (a, b, d) in ((lt[0], lt[1], d1), (lt[2], lt[3], d2)):
                    kb.op("dve", lambda a=a, b=b: nc.vector.tensor_tensor(out=a[:, :], in0=a[:, :], in1=b[:, :], op=ALU.mult),
                          reads=[a, b], writes=[a])
                    kb.op("dve", lambda a=a, d=d: nc.vector.tensor_reduce(out=d[:, 0:1], in_=a[:, :], axis=AX.X, op=ALU.add),
                          reads=[a], writes=[d])
                    kb.op("act", lambda d=d: nc.scalar.activation(out=d[:, 0:1], in_=d[:, 0:1], func=AF.Exp), reads=[d], writes=[d])
                kb.op("dve", lambda: nc.vector.scalar_tensor_tensor(out=neglam[:, 0:1], in0=d2[:, 0:1], scalar=-lam_init, in1=d1[:, 0:1],
                                                                    op0=ALU.add, op1=ALU.subtract), reads=[d1, d2], writes=[neglam])
                subw = kb.sb(st, "subw", [128, 256], F32)
                s.bc_load(subw, p["subln"][0:1, :])
                kb.op("dve", lambda: nc.vector.tensor_scalar(out=subw[:, :], in0=subw[:, :], scalar1=1.0 - lam_init, scalar2=None,
                                                             op0=ALU.mult), reads=[subw], writes=[subw])
                o1 = kb.sb(st, "o1", [128, 4, 256], F32)
                od = [kb.sb(st, "od", [128, 256], F32) for _ in range(2)]
                junk = kb.sb(st, "ajunk", [128, 256], BF16)
                ssq = kb.sb(st, "assq", [128, 1], F32)
                rstd = kb.sb(st, "arstd", [128, 1], F32)
            nkb = 2 if l == 0 else 1
            kTs = [kb.sb(st, "kT", [128, TT], BF16) for _ in range(2 * nkb)]
            Vaug = [kb.sb(st, "Vaug", [128, NT, dv + 1], BF16) for _ in range(2)]
            for V in Vaug:
                kb.op("dve", lambda V=V: nc.vector.memset(V[:, :, :], 1.0), writes=[V])
            qTs = [kb.sb(st, "qT", [128, 512], BF16) for _ in range(3)]
            PTs = [kb.sb(st, "PT", [128, 512], BF16) for _ in range(3)]
            rc = [kb.sb(st, "rc", [128, 1], F32) for _ in range(4)]
            ob = [kb.sb(st, "ob", [128, dv], BF16) for _ in range(4)]
            nch = dv // 128
            ostage = [kb.sb(st, "ostage", [128, nch, 512], BF16) for _ in range(3)]
            pS = [kb.ps(st, "pS", [128, 512], F32) for _ in range(2)]
            pO = [kb.ps(st, "pO", [128, 512], F32) for _ in range(4)]
            pT = [kb.ps(st, "pT", [128, 8, 128], BF16) for _ in range(1 if bg is not None else 2)]
            if l == 0:
                odq = [kb.sb(st, "odq", [128, 256], F32) for _ in range(4)]
                sq = kb.sb(st, "asq", [128, 256], F32)
                ssqq = [kb.sb(st, "assq", [128, 1], F32) for _ in range(4)]
                rstq = [kb.sb(st, "arst", [128, 1], F32) for _ in range(4)]

            qgroups = [(a, k, list(range(NT))) for (a, k) in split_groups(NTL, 4)]
            if l == 0:
                qgroups.append((NTL, NTC, list(range(NTL, NT))))
            ngrp = cfg.H0 if l == 0 else cfg.KV1
            cnt = {"q": 0, "pt": 0, "os": 0, "pT": 0}
            pending = []

            def flush(upto=None):
                while pending and (upto is None or pending[0][0] <= upto):
                    pending.pop(0)[1]()

            for g in range(ngrp):
                kset = kTs[(g % 2) * nkb:(g % 2) * nkb + nkb]
                V = Vaug[g % 2]
                if l == 0:
                    kblocks = [nq + 2 * g, nq + 2 * g + 1]
                    units = [(2 * g, 0), (2 * g + 1, 1)]
                    vcol0 = g * 256
                else:
                    kblocks = [nq + g]
                    units = [(4 * g + i, 0) for i in range(4)]
                    vcol0 = g * 128
                for i, kbk in enumerate(kblocks):
                    kb.dma("sp", [lambda e, i=i, kbk=kbk: e.dma_start(out=kset[i][:, :], in_=s.qkT[kbk, :, :])], writes=[kset[i]])
                kb.dma("sp", [lambda e: e.dma_start(out=V[:, :, 0:dv],
                                                    in_=s.v[:, vcol0:vcol0 + dv].rearrange("(t p) e -> p t e", p=128))], writes=[V])
                for (qt0, nqt, keyt) in qgroups:
                    n = nqt * 128
                    for ui, (qblk, kidx) in enumerate(units):
                        qT = qTs[cnt["q"] % 3]
                        cnt["q"] += 1
                        kb.dma("sp", [lambda e: e.dma_start(out=qT[:, 0:n], in_=s.qkT[qblk, :, qt0 * 128:qt0 * 128 + n])], writes=[qT])
                        kT = kset[kidx]
                        nk = len(keyt)

                        def emit_S(ki):
                            kt = keyt[ki]
                            ps_ = pS[ki % 2]
                            kb.op("pe", lambda: nc.tensor.matmul(ps_[:, 0:n], lhsT=kT[:, kt * 128:(kt + 1) * 128], rhs=qT[:, 0:n],
                                                                 start=True, stop=True), reads=[kT, qT], writes=[ps_])
                        emit_S(0)
                        for ki in range(nk):
                            kt = keyt[ki]
                            if ki + 1 < nk:
                                emit_S(ki + 1)
                            ps_ = pS[ki % 2]
                            PT = PTs[cnt["pt"] % 3]
                            cnt["pt"] += 1
                            kb.op("act", lambda: nc.scalar.activation(out=PT[:, 0:n], in_=ps_[:, 0:n], func=AF.Exp, scale=scale,
                                                                      bias=negC[:, 0:1]), reads=[ps_, negC], writes=[PT])
                            for qi in range(nqt):
                                kb.op("pe", lambda qi=qi: nc.tensor.matmul(pO[qi][:, 0:dv + 1], lhsT=PT[:, qi * 128:(qi + 1) * 128], rhs=V[:, kt, :],
                                                                          start=(ki == 0), stop=(ki == nk - 1)),
                                      reads=[PT, V], writes=[pO[qi]], inc=(ki == nk - 1 or qi == nqt - 1))
                            if ki >= 2 and (ki - 2) % 3 == 0:
                                flush(upto=(ki - 2) // 3)
                        flush()
                        if bg is not None:
                            bg()
                        has_out = (l == 1) or (ui == 1)
                        if has_out:
                            ost = ostage[cnt["os"] % 3]
                            cnt["os"] += 1
                        for qi in range(nqt):
                            O = pO[qi]
                            r_ = rc[qi]
                            kb.op("dve", lambda: nc.vector.reciprocal(out=r_[:, 0:1], in_=O[:, dv:dv + 1]), reads=[O], writes=[r_])
                            if l == 1:
                                o_b = ob[qi]
                                kb.op("dve", lambda: nc.vector.tensor_scalar(out=o_b[:, :], in0=O[:, 0:dv], scalar1=r_[:, 0:1], scalar2=None,
                                                                             op0=ALU.mult), reads=[O, r_], writes=[o_b])
                            elif ui == 0:
                                kb.op("dve", lambda: nc.vector.tensor_scalar(out=o1[:, qi, :], in0=O[:, 0:dv], scalar1=r_[:, 0:1], scalar2=None,
                                                                             op0=ALU.mult), reads=[O, r_], writes=[o1])
                            else:
                                od_ = odq[qi]
                                kb.op("dve", lambda: nc.vector.tensor_tensor(out=r_[:, 0:1], in0=r_[:, 0:1], in1=neglam[:, 0:1], op=ALU.mult),
                                      reads=[r_, neglam], writes=[r_])
                                kb.op("dve", lambda: nc.vector.scalar_tensor_tensor(out=od_[:, :], in0=O[:, 0:dv], scalar=r_[:, 0:1],
                                                                                    in1=o1[:, qi, :], op0=ALU.mult, op1=ALU.add),
                                      reads=[O, r_, o1], writes=[od_])
                                kb.op("dve", lambda: nc.vector.tensor_tensor(out=sq[:, :], in0=od_[:, :], in1=od_[:, :], op=ALU.mult),
                                      reads=[od_], writes=[sq])
                                kb.op("dve", lambda: nc.vector.tensor_reduce(out=ssqq[qi][:, 0:1], in_=sq[:, :], axis=AX.X, op=ALU.add),
                                      reads=[sq], writes=[ssqq[qi]])
                        if not has_out:
                            continue

                        def make_Eb(qi, ost=ost, qblk=qblk, g=g, qt0=qt0, n=n, nqt=nqt):
                            def fn():
                                o_b = ob[qi]
                                if l == 0:
                                    kb.op("act", lambda: nc.scalar.activation(out=rstq[qi][:, 0:1], in_=ssqq[qi][:, 0:1], func=AF.Ln,
                                                                              scale=1.0 / 256, bias=s.eps_col[:, 0:1]),
                                          reads=[ssqq[qi], s.eps_col], writes=[rstq[qi]])
                                    kb.op("act", lambda: nc.scalar.activation(out=rstq[qi][:, 0:1], in_=rstq[qi][:, 0:1], func=AF.Exp, scale=-0.5),
                                          reads=[rstq[qi]], writes=[rstq[qi]])
                                    kb.op("dve", lambda: nc.vector.scalar_tensor_tensor(out=o_b[:, :], in0=odq[qi][:, :], scalar=rstq[qi][:, 0:1],
                                                                                        in1=subw[:, :], op0=ALU.mult, op1=ALU.mult),
                                          reads=[odq[qi], rstq[qi], subw], writes=[o_b])
                                pt_ = pT[cnt["pT"] % len(pT)]
                                cnt["pT"] += 1
                                for c in range(nch):
                                    kb.op("pe", lambda c=c: nc.tensor.transpose(out=pt_[:, c, :], in_=o_b[:, c * 128:(c + 1) * 128],
                                                                                identity=s.ident_b[:, :]),
                                          reads=[o_b, s.ident_b], writes=[pt_], inc=(c == nch - 1))
                                kb.op("dve", lambda: nc.vector.tensor_copy(out=ost[:, :, qi * 128:(qi + 1) * 128], in_=pt_[:, 0:nch, :]),
                                      reads=[pt_], writes=[ost])
                                if qi == nqt - 1:
                                    if l == 1:
                                        kb.dma("sp", [lambda e: e.dma_start(out=s.oT[:, qblk, qt0 * 128:qt0 * 128 + n], in_=ost[:, 0, 0:n])],
                                               reads=[ost])
                                    else:
                                        kb.dma("sp", [lambda e: e.dma_start(out=s.oT[:, 2 * g:2 * g + 2, qt0 * 128:qt0 * 128 + n],
                                                                            in_=ost[:, :, 0:n])], reads=[ost])
                            return fn
                        for qi in range(nqt):
                            pending.append((qi, make_Eb(qi)))
            flush()

    def phase_wo(s, l):
        kb, nc, cfg = s.kb, s.nc, s.cfg
        D, KC, NT, NTL = cfg.D, cfg.KC, cfg.NT, cfg.NTL
        p = s.L[l]
        ntiles = NT if l == 0 else NTL
        TB = split_groups(ntiles, 9)
        mxnt = max(k for _, k in TB)
        with ExitStack() as st:
            oT_sb = kb.sb(st, "oT_sb", [128, KC, mxnt * 128], BF16)
            wb = [kb.sb(st, "wob", [128, KC, 512], BF16) for _ in range(2)]
            g1 = [[kb.sb(st, "g1", [128, 512], F32) for _ in range(2)] for _ in range(2)]
            xt = [kb.sb(st, "wxt", [128, 512], F32) for _ in range(3)]
            tmp = [kb.sb(st, "wtmp", [128, 512], F32) for _ in range(2)]
            xo = [kb.sb(st, "wxo", [128, 512], F32) for _ in range(3)]
            pacc = [kb.ps(st, "wacc", [128, 512], F32) for _ in range(2)]
            NB = D // 512
            it = 0
            wi = 0
            for (t0, nt) in TB:
                kstep = min(8, KC)
                for k0 in range(0, KC, kstep):
                    kb.dma("sp", [lambda e, k0=k0: e.dma_start(out=oT_sb[:, k0:k0 + kstep, 0:nt * 128],
                                                                in_=s.oT[:, k0:k0 + kstep, t0 * 128:(t0 + nt) * 128])],
                           writes=[oT_sb])
                s.load_w_block("pool", wb[wi % 2], p["wo"], 0, 512)
                for nb in range(NB):
                    W = wb[wi % 2]
                    wi += 1
                    if nb + 1 < NB:
                        s.load_w_block("pool", wb[wi % 2], p["wo"], (nb + 1) * 512, 512)
                    gl = g1[nb % 2]
                    nr = 2 if l == 0 else 1
                    for r in range(nr):
                        s.bc_load(gl[r], s.mod_row(l, r, 2)[0:1, nb * 512:(nb + 1) * 512])
                    for ti in range(nt):
                        gt = t0 + ti
                        gg = gl[0 if gt < NTL else 1]
                        x_, tm, xo_, acc = xt[it % 3], tmp[it % 2], xo[it % 3], pacc[it % 2]
                        it += 1
                        kb.dma("sp", [lambda e: e.dma_start(out=x_[:, :], in_=s.x_src(l, "in", gt)[:, nb * 512:(nb + 1) * 512])], writes=[x_])
                        for kc in range(KC):
                            kb.op("pe", lambda kc=kc: nc.tensor.matmul(acc[:, :], lhsT=oT_sb[:, kc, ti * 128:(ti + 1) * 128], rhs=W[:, kc, :],
                                                                       start=(kc == 0), stop=(kc == KC - 1)),
                                  reads=[oT_sb, W], writes=[acc], inc=(kc == KC - 1))
                        kb.op("dve", lambda: nc.vector.tensor_tensor(out=tm[:, :], in0=acc[:, :], in1=gg[:, :], op=ALU.mult),
                              reads=[acc, gg], writes=[tm])
                        kb.op("dve", lambda: nc.vector.tensor_tensor(out=xo_[:, :], in0=tm[:, :], in1=x_[:, :], op=ALU.add),
                              reads=[tm, x_], writes=[xo_])
                        kb.dma("sp", [lambda e: e.dma_start(out=s.x_dst(l, gt)[:, nb * 512:(nb + 1) * 512], in_=xo_[:, :])], reads=[xo_])

    def phase_moe(s, l):
        kb, nc, cfg = s.kb, s.nc, s.cfg
        D, KC, NT, NTL, T, C, E, FF, FC = cfg.D, cfg.KC, cfg.NT, cfg.NTL, cfg.T, cfg.C, cfg.E, cfg.FF, cfg.FC
        p = s.L[l]
        has_ctx = (l == 0)
        ntiles = NT if has_ctx else NTL
        NB = D // 512
        CAPL, CAPC = cfg.CAPL, cfg.CAPC
        groups = [(a, min(128, CAPL), 0) for a in range(0, CAPL, 128)]
        if has_ctx:
            groups.append((0, CAPC, 1))
        S = sum(g[1] for g in groups)
        tgt = (s.xres if l == 0 else s.out).rearrange("t (b c) -> (t b) c", c=512)
        with ExitStack() as st0:
            idxT = [kb.sb(st0, "idxT", [128, NB + 1, E], U32) for _ in groups]
            gateT = [kb.sb(st0, "gateT", [128, E], F32) for _ in groups]
            st_aff = ExitStack()
            affT = [kb.sb(st_aff, "affT", [E, T], F32)]
            if has_ctx:
                affT.append(kb.sb(st_aff, "affTc", [E, C], F32))
            with ExitStack() as st:
                with ExitStack() as stg:
                    AB = s.norm_setup(st, l, 2, stg)
                    kb.barrier()
                rt = kb.sb(st, "router", [128, KC, E], F32)
                kstep = min(8, KC)
                kb.dma("sp", [lambda e, k0=k0: e.dma_start(out=rt[:, k0:k0 + kstep, :],
                                                            in_=p["router"].rearrange("(k p) e -> p k e", p=128)[:, k0:k0 + kstep, :])
                              for k0 in range(0, KC, kstep)], writes=[rt])
                xt = [kb.sb(st, "xt2", [128, D], F32) for _ in range(2)]
                hb = [kb.sb(st, "hb2", [128, D], BF16) for _ in range(2)]
                junk = kb.sb(st, "junk2", [128, D], BF16)
                ssq = kb.sb(st, "ssq2", [128, 1], F32)
                rstd = kb.sb(st, "rstd2", [128, 1], F32)
                h2T = kb.sb(st, "h2T", [128, KC, 128], F32)
                mx = kb.sb(st, "mx", [128, 1], F32)
                se = kb.sb(st, "se", [128, 1], F32)
                ex = kb.sb(st, "ex", [128, E], F32)
                aff = kb.sb(st, "aff", [128, E], F32)
                ptf = [kb.ps(st, "ptf", [128, 4, 128], F32) for _ in range(3)]
                plg = kb.ps(st, "plg", [128, 512], F32)
                pat = kb.ps(st, "pat", [128, 512], F32)
                pi = [0]

                def stage1(ti):
                    lat = ti < NTL
                    A, B = AB[0 if lat else 1]
                    x_t, h_b = xt[ti % 2], hb[ti % 2]
                    kb.dma("sp", [lambda e: e.dma_start(out=x_t[:, :], in_=s.x_src(l, "mid", ti))], writes=[x_t])
                    s.norm_tile(x_t, A, B, ssq, rstd, junk, x_t, x_t)
                    kb.op("act", lambda: nc.scalar.copy(out=h_b[:, :], in_=x_t[:, :]), reads=[x_t], writes=[h_b])
                    kb.dma("sp", [lambda e: e.dma_start(out=s.h2[ti * 128:(ti + 1) * 128, :], in_=h_b[:, :])], reads=[h_b])

                def stage2(ti):
                    lat = ti < NTL
                    x_t = xt[ti % 2]
                    NPB = min(4, KC)
                    for k0 in range(0, KC, NPB):
                        pt = ptf[pi[0] % 3]
                        pi[0] += 1
                        for j in range(NPB):
                            kb.op("pe", lambda j=j: nc.tensor.transpose(out=pt[:, j, :], in_=x_t[:, (k0 + j) * 128:(k0 + j + 1) * 128],
                                                                        identity=s.ident_f[:, :]),
                                  reads=[x_t, s.ident_f], writes=[pt], inc=(j == NPB - 1))
                        kb.op("dve", lambda: nc.vector.tensor_copy(out=h2T[:, k0:k0 + NPB, :], in_=pt[:, 0:NPB, :]), reads=[pt], writes=[h2T])
                    for kc in range(KC):
                        kb.op("pe", lambda kc=kc: nc.tensor.matmul(plg[:, 0:E], lhsT=h2T[:, kc, :], rhs=rt[:, kc, :],
                                                                   start=(kc == 0), stop=(kc == KC - 1)),
                              reads=[h2T, rt], writes=[plg], inc=(kc == KC - 1))
                    kb.op("dve", lambda: nc.vector.tensor_reduce(out=mx[:, 0:1], in_=plg[:, 0:E], axis=AX.X, op=ALU.max),
                          reads=[plg], writes=[mx])
                    kb.op("dve", lambda: nc.vector.tensor_scalar(out=mx[:, 0:1], in0=mx[:, 0:1], scalar1=-1.0, scalar2=None, op0=ALU.mult),
                          reads=[mx], writes=[mx])
                    kb.op("act", lambda: nc.scalar.activation(out=ex[:, :], in_=plg[:, 0:E], func=AF.Exp, bias=mx[:, 0:1]),
                          reads=[plg, mx], writes=[ex])
                    kb.op("dve", lambda: nc.vector.tensor_reduce(out=se[:, 0:1], in_=ex[:, :], axis=AX.X, op=ALU.add),
                          reads=[ex], writes=[se])
                    kb.op("dve", lambda: nc.vector.reciprocal(out=se[:, 0:1], in_=se[:, 0:1]), reads=[se], writes=[se])
                    kb.op("dve", lambda: nc.vector.tensor_scalar(out=aff[:, :], in0=ex[:, :], scalar1=se[:, 0:1], scalar2=None, op0=ALU.mult),
                          reads=[ex, se], writes=[aff])
                    kb.op("pe", lambda: nc.tensor.transpose(out=pat[0:E, 0:128], in_=aff[:, :], identity=s.ident_f[:, :]),
                          reads=[aff, s.ident_f], writes=[pat])
                    dst = affT[0] if lat else affT[1]
                    c0 = (ti if lat else ti - NTL) * 128
                    kb.op("dve", lambda: nc.vector.tensor_copy(out=dst[0:E, c0:c0 + 128], in_=pat[0:E, 0:128]), reads=[pat], writes=[dst])

                stage1(0)
                for ti in range(ntiles):
                    if ti + 1 < ntiles:
                        stage1(ti + 1)
                    stage2(ti)
                kb.barrier()
            if s.dbg:
                kb.dma("sp", [lambda e: e.dma_start(out=s.d_aff[:, :], in_=affT[0][:, :])], reads=[affT[0]])
            if getattr(s, "moe_stop", None) == "A":
                st_aff.close()
                return
            with ExitStack() as st:
                pidx = kb.ps(st, "pidx", [128, 512], F32)
                for kind in range(2 if has_ctx else 1):
                    n = T if kind == 0 else C
                    cap = CAPL if kind == 0 else CAPC
                    work = kb.sb(st, "work", [E, n], F32)
                    vals = kb.sb(st, "vals", [E, cap], F32)
                    idx = kb.sb(st, "idx", [E, cap], U32)
                    idxf = kb.sb(st, "idxf", [E, cap], F32)
                    kb.op("dve", lambda: nc.vector.tensor_copy(out=work[:, :], in_=affT[kind][:, :]), reads=[affT[kind]], writes=[work])
                    for it in range(cap // 8):
                        sl = slice(it * 8, (it + 1) * 8)
                        kb.op("dve", lambda: nc.vector.max(out=vals[:, sl], in_=work[:, :]), reads=[work], writes=[vals])
                        kb.op("dve", lambda: nc.vector.max_index(out=idx[:, sl], in_max=vals[:, sl], in_values=work[:, :]),
                              reads=[vals, work], writes=[idx])
                        kb.op("dve", lambda: nc.vector.match_replace(out=work[:, :], in_to_replace=vals[:, sl], in_values=work[:, :],
                                                                     imm_value=-1.0), reads=[vals, work], writes=[work])
                    kb.op("dve", lambda: nc.vector.tensor_copy(out=idxf[:, :], in_=idx[:, :]), reads=[idx], writes=[idxf])
                    if kind == 1:
                        kb.op("dve", lambda: nc.vector.tensor_scalar(out=idxf[:, :], in0=idxf[:, :], scalar1=float(T), scalar2=None, op0=ALU.add),
                              reads=[idxf], writes=[idxf])
                    for gi, (s0, ns, kd) in enumerate(groups):
                        if kd != kind:
                            continue
                        tf = kb.sb(st, "tf", [128, E], F32)
                        tf2 = kb.sb(st, "tf2", [128, E], F32)
                        kb.op("pe", lambda: nc.tensor.transpose(out=pidx[0:ns, 0:E], in_=idxf[0:E, s0:s0 + ns], identity=s.ident_f[0:E, 0:E]),
                              reads=[idxf, s.ident_f], writes=[pidx])
                        kb.op("dve", lambda: nc.vector.tensor_copy(out=tf[0:ns, :], in_=pidx[0:ns, 0:E]), reads=[pidx], writes=[tf])
                        kb.op("dve", lambda: nc.vector.tensor_copy(out=idxT[gi][0:ns, 0, :], in_=tf[0:ns, :]), reads=[tf], writes=[idxT[gi]])
                        for nb in range(NB):
                            kb.op("dve", lambda nb=nb: nc.vector.tensor_scalar(out=tf2[0:ns, :], in0=tf[0:ns, :], scalar1=float(NB), scalar2=float(nb),
                                                                               op0=ALU.mult, op1=ALU.add), reads=[tf], writes=[tf2])
                            kb.op("dve", lambda nb=nb: nc.vector.tensor_copy(out=idxT[gi][0:ns, nb + 1, :], in_=tf2[0:ns, :]),
                                  reads=[tf2], writes=[idxT[gi]])
                        kb.op("pe", lambda: nc.tensor.transpose(out=pidx[0:ns, 0:E], in_=vals[0:E, s0:s0 + ns], identity=s.ident_f[0:E, 0:E]),
                              reads=[vals, s.ident_f], writes=[pidx])
                        kb.op("dve", lambda: nc.vector.tensor_copy(out=gateT[gi][0:ns, :], in_=pidx[0:ns, 0:E]), reads=[pidx], writes=[gateT[gi]])
                kb.barrier()
            if s.dbg:
                kb.dma("sp", [lambda e: e.dma_start(out=s.d_idx[:, :, :], in_=idxT[0][:, :, :])], reads=[idxT[0]])
                kb.dma("sp", [lambda e: e.dma_start(out=s.d_gate[:, :], in_=gateT[0][:, :])], reads=[gateT[0]])
            st_aff.close()
            if getattr(s, "moe_stop", None) == "B":
                return
            with ExitStack() as st:
                nkind = 2 if has_ctx else 1
                g2s = [[kb.sb(st, "g2s", [128, 512], F32) for _ in range(2)] for _ in range(nkind)]
                FB = min(256, FF)
                NSUB = FB // 128
                xs = [kb.sb(st, "xs", [128, D], BF16) for _ in groups]
                xsT = [kb.sb(st, "xsT", [128, KC, S], BF16) for _ in range(2)]
                MP = 128 if has_ctx else 0
                SP = S if not has_ctx else (S - CAPC + MP)
                hidT = [kb.sb(st, "hidT", [128, FC, SP], BF16) for _ in range(2)]
                for h_ in hidT:
                    kb.op("dve", lambda h_=h_: nc.vector.memset(h_[:, :, :], 0.0), writes=[h_])
                Wg = [kb.sb(st, "Wg", [128, KC, FB], BF16) for _ in range(2)]
                Wu = [kb.sb(st, "Wu", [128, KC, FB], BF16) for _ in range(2)]
                Wd = [kb.sb(st, "Wd", [128, FC, 512], BF16) for _ in range(2)]
                sg = [kb.sb(st, "sg", [128, S], F32) for _ in range(2)]
                ys = [kb.sb(st, "ys", [128, 512], F32) for _ in range(2)]
                ptx = [kb.ps(st, "ptx", [128, 8, 128], BF16) for _ in range(2)]
                pg = [kb.ps(st, "pg", [128, 512], F32) for _ in range(2)]
                pu = [kb.ps(st, "pu", [128, 512], F32) for _ in range(2)]
                py = [kb.ps(st, "py", [128, 512], F32) for _ in range(2)]
                tres = [Res() for _ in range(NB)]
                soff = []
                a = 0
                for (_, ns, _) in groups:
                    soff.append(a)
                    a += ns
                cn = {"w": 0, "px": 0, "f": 0, "d": 0, "y": 0, "py": 0}

                KS_GU = max(1, min(KC, 1024 // FB))
                KS_D = max(1, min(FC, 2))
                stg_gu = [kb.sb(st, "stg_gu", [128, KS_GU, FB], F32) for _ in range(6)]
                stg_d = [kb.sb(st, "stg_d", [128, KS_D, 512], F32) for _ in range(3)]

                def load_cast(tile, w_ap, c0, ncols, eng, ring, key, ks):
                    src = w_ap[:, c0:c0 + ncols].rearrange("(k p) n -> p k n", p=128)
                    KCn = src.shape[1]
                    for k0 in range(0, KCn, ks):
                        k1 = min(KCn, k0 + ks)
                        sg_ = ring[cn[key] % len(ring)]
                        cn[key] += 1
                        kb.dma("sp", [lambda e_: e_.dma_start(out=sg_[:, 0:k1 - k0, :], in_=src[:, k0:k1, :])], writes=[sg_])
                        if eng == "act":
                            kb.op("act", lambda: nc.scalar.copy(out=tile[:, k0:k1, 0:ncols], in_=sg_[:, 0:k1 - k0, :]), reads=[sg_], writes=[tile])
                        else:
                            kb.op("dve", lambda: nc.vector.tensor_copy(out=tile[:, k0:k1, 0:ncols], in_=sg_[:, 0:k1 - k0, :]),
                                  reads=[sg_], writes=[tile])

                cn["sgu"] = 0
                cn["sd"] = 0

                def load_gu(e, fb, slot):
                    load_cast(Wg[slot], p["wg"][e], fb * FB, FB, "act", stg_gu, "sgu", KS_GU)
                    load_cast(Wu[slot], p["wu"][e], fb * FB, FB, "dve", stg_gu, "sgu", KS_GU)

                def load_d(e, nb, slot):
                    load_cast(Wd[slot], p["wd"][e], nb * 512, 512, "act" if slot == 0 else "dve", stg_d, "sd", KS_D)

                def gather(e):
                    for gi, (s0, ns, kd) in enumerate(groups):
                        kb.dma("pool", [lambda eng, gi=gi, ns=ns: eng.indirect_dma_start(
                            out=xs[gi][0:ns, :], out_offset=None, in_=s.h2[:, :],
                            in_offset=bass.IndirectOffsetOnAxis(ap=idxT[gi][0:ns, 0, e:e + 1], axis=0))],
                            reads=[idxT[gi]], writes=[xs[gi]])

                def transposes(e):
                    xT = xsT[e % 2]
                    for gi, (s0, ns, kd) in enumerate(groups):
                        NPB = min(8, KC)
                        for k0 in range(0, KC, NPB):
                            pt = ptx[cn["px"] % 2]
                            cn["px"] += 1
                            for j in range(NPB):
                                kb.op("pe", lambda j=j: nc.tensor.transpose(out=pt[:, j, 0:ns], in_=xs[gi][0:ns, (k0 + j) * 128:(k0 + j + 1) * 128],
                                                                            identity=s.ident_b[0:ns, 0:ns]),
                                      reads=[xs[gi], s.ident_b], writes=[pt], inc=(j == NPB - 1))
                            kb.op("act", lambda: nc.scalar.copy(out=xT[:, k0:k0 + NPB, soff[gi]:soff[gi] + ns], in_=pt[:, 0:NPB, 0:ns]),
                                  reads=[pt], writes=[xT])

                def gateup(e):
                    xT = xsT[e % 2]
                    hT_ = hidT[e % 2]
                    nfb = FF // FB
                    for fb in range(nfb):
                        slot = cn["w"] % 2
                        cn["w"] += 1
                        if fb + 1 < nfb:
                            load_gu(e, fb + 1, cn["w"] % 2)
                        if fb == nfb - 1:
                            load_d(e, 0, cn["d"] % 2)
                        for sub in range(NSUB):
                            fc = fb * NSUB + sub
                            pg_, pu_, sg_ = pg[cn["f"] % 2], pu[cn["f"] % 2], sg[cn["f"] % 2]
                            cn["f"] += 1
                            for (W_, pp) in ((Wg[slot], pg_), (Wu[slot], pu_)):
                                for kc in range(KC):
                                    kb.op("pe", lambda kc=kc: nc.tensor.matmul(pp[:, 0:S], lhsT=W_[:, kc, sub * 128:(sub + 1) * 128], rhs=xT[:, kc, 0:S],
                                                                               start=(kc == 0), stop=(kc == KC - 1)),
                                          reads=[W_, xT], writes=[pp], inc=(kc == KC - 1))
                            kb.op("act", lambda: nc.scalar.activation(out=sg_[:, 0:S], in_=pg_[:, 0:S], func=AF.Silu), reads=[pg_], writes=[sg_])
                            kb.op("dve", lambda: nc.vector.tensor_tensor(out=hT_[:, fc, 0:S], in0=sg_[:, 0:S], in1=pu_[:, 0:S], op=ALU.mult),
                                  reads=[sg_, pu_], writes=[hT_])

                def down(e):
                    hT_ = hidT[e % 2]
                    for nb in range(NB):
                        Wd_ = Wd[cn["d"] % 2]
                        cn["d"] += 1
                        if nb + 1 < NB:
                            load_d(e, nb + 1, cn["d"] % 2)
                        g2 = [g2s[r][cn["d"] % 2] for r in range(nkind)]
                        for r in range(nkind):
                            s.bc_load(g2[r], s.mod_row(l, r, 5)[0:1, nb * 512:(nb + 1) * 512])
                        for gi, (s0, ns, kd) in enumerate(groups):
                            py_ = py[cn["py"] % 2]
                            cn["py"] += 1
                            ys_ = ys[cn["y"] % 2]
                            cn["y"] += 1
                            nsm = ns if kd == 0 else MP
                            for fc in range(FC):
                                kb.op("pe", lambda fc=fc: nc.tensor.matmul(py_[0:nsm, :], lhsT=hT_[:, fc, soff[gi]:soff[gi] + nsm], rhs=Wd_[:, fc, :],
                                                                           start=(fc == 0), stop=(fc == FC - 1)),
                                      reads=[hT_, Wd_], writes=[py_], inc=(fc == FC - 1))
                            kb.op("dve", lambda: nc.vector.scalar_tensor_tensor(out=ys_[0:ns, :], in0=py_[0:ns, :], scalar=gateT[gi][0:ns, e:e + 1],
                                                                                in1=g2[kd][0:ns, :], op0=ALU.mult, op1=ALU.mult),
                                  reads=[py_, gateT[gi], g2[kd]], writes=[ys_])
                            kb.dma("pool", [lambda eng: eng.indirect_dma_start(
                                out=tgt, out_offset=bass.IndirectOffsetOnAxis(ap=idxT[gi][0:ns, nb + 1, e:e + 1], axis=0),
                                in_=ys_[0:ns, :], in_offset=None, compute_op=ALU.add)],
                                reads=[ys_, idxT[gi]], writes=[tres[nb]])

                gather(0)
                load_gu(0, 0, cn["w"] % 2)
                transposes(0)
                for e in range(E):
                    gateup(e)
                    if e + 1 < E:
                        gather(e + 1)
                        load_gu(e + 1, 0, cn["w"] % 2)
                    down(e)
                    if e + 1 < E:
                        transposes(e + 1)

def host_consts(cfg):
    T, TT = cfg.T, cfg.TT
    rows = T // cfg.GW
    row = np.repeat(np.arange(rows, dtype=np.float32), cfg.GW)
    col = np.tile(np.arange(cfg.GW, dtype=np.float32), rows)
    inv_freq = (10000.0 ** (-np.arange(0, 64, 2, dtype=np.float32) / 64)).astype(np.float32)
    ang = np.concatenate([row[:, None] * inv_freq, col[:, None] * inv_freq], axis=-1)
    cos = np.ones((128, TT), np.float32)
    sin = np.zeros((128, TT), np.float32)
    cos[:64, :T] = np.cos(ang).T
    cos[64:, :T] = np.cos(ang).T
    sin[:64, :T] = np.sin(ang).T
    sin[64:, :T] = np.sin(ang).T
    ident = np.eye(128, dtype=np.float32)
    rot = np.zeros((128, 128), np.float32)
    for i in range(64):
        rot[i + 64, i] = -1.0
        rot[i, i + 64] = 1.0
    return {"k_rope_c": cos, "k_rope_s": sin, "k_ident": ident, "k_rot": rot}


def make_in_maps(cfg, inputs, n_cores):
    consts = host_consts(cfg)
    shared = dict(consts)
    shared["c_ctx"] = np.ascontiguousarray(inputs["c_ctx"]).reshape(1, -1)
    for k, v in inputs.items():
        if k in ("x", "c", "ctx", "c_ctx"):
            continue
        a = np.ascontiguousarray(v)
        if a.ndim == 1:
            a = a.reshape(1, -1)
        shared[k] = a
    maps = []
    for b in range(n_cores):
        m = dict(shared)
        m["x"] = np.ascontiguousarray(inputs["x"][b])
        m["c"] = np.ascontiguousarray(inputs["c"][b]).reshape(1, -1)
        m["ctx"] = np.ascontiguousarray(inputs["ctx"][b])
        maps.append(m)
    return maps


def kernel(**inputs):
    cfg = Cfg()
    prog = Prog(cfg)
    n = inputs["x"].shape[0]
    maps = make_in_maps(cfg, inputs, n)
    res = run_bass_kernel_spmd(prog.nc, maps, core_ids=list(range(n)))
    return np.stack([np.asarray(r["out"]) for r in res.results], axis=0).astype(np.float32)
```

```python
import math
from contextlib import ExitStack

import numpy as np
import concourse.bass as bass
import concourse.mybir as mybir
from concourse.bass_utils import run_bass_kernel_spmd

F32 = mybir.dt.float32
BF16 = mybir.dt.bfloat16
U32 = mybir.dt.uint32
AF = mybir.ActivationFunctionType
ALU = mybir.AluOpType
AX = mybir.AxisListType

NORM_EPS = 1e-6
N_EXPERTS = 16


class Cfg:
    def __init__(s, D=4096, T=2048, C=256, FF=1024, GW=64):
        s.D, s.T, s.C, s.FF, s.GW = D, T, C, FF, GW
        s.KC = D // 128
        s.TT = T + C
        s.NT = s.TT // 128
        s.NTL = T // 128
        s.NTC = C // 128
        s.H0 = D // 256
        s.H1 = D // 128
        s.KV1 = s.H1 // 4
        s.KVD = s.KV1 * 128
        s.E = N_EXPERTS
        s.CAPL = 2 * T // s.E
        s.CAPC = 2 * C // s.E
        s.FC = FF // 128


class Tile:
    def __init__(s, h):
        s.h = h
        s.w = None
        s.r = []

    def __getitem__(s, k):
        return s.h[k]


class Res(Tile):
    def __init__(s):
        s.h = None
        s.w = None
        s.r = []


class KB:
    NQ = 8

    def __init__(s, nc):
        s.nc = nc
        s.E = {"pe": nc.tensor, "act": nc.scalar, "dve": nc.vector, "pool": nc.gpsimd, "sp": nc.sync}
        s.csem = {e: nc.alloc_semaphore(name=f"cs_{e}") for e in ("pe", "act", "dve", "pool")}
        s.cnt = dict.fromkeys(s.csem, 0)
        s.waited = {e: {} for e in s.E}
        s.dq = {}
        for q in ("sp", "pool"):
            s.dq[q] = {"sems": [nc.alloc_semaphore(name=f"dq_{q}{i}") for i in range(s.NQ)],
                       "vals": [0] * s.NQ, "i": 0}
        s.pending = []
        s.uid = 0

    def sb(s, st, name, shape, dt):
        s.uid += 1
        return Tile(st.enter_context(s.nc.sbuf_tensor(f"{name}_{s.uid}", list(shape), dt)))

    def ps(s, st, name, shape, dt=F32):
        s.uid += 1
        return Tile(st.enter_context(s.nc.psum_tensor(f"{name}_{s.uid}", list(shape), dt)))

    def _wait(s, e, tok):
        if tok is None:
            return
        sem, val, owner = tok
        if owner == e:
            if e == "pe":
                return
            if val > s.cnt[e]:
                return
        key = sem.name
        if s.waited[e].get(key, 0) >= val:
            return
        s.E[e].wait_ge(sem, val)
        s.waited[e][key] = val

    def _deps(s, e, reads, writes):
        for r in reads:
            s._wait(e, r.w)
        for w in writes:
            s._wait(e, w.w)
            for t in w.r:
                s._wait(e, t)

    def _commit(s, tok, reads, writes):
        for r in reads:
            r.r.append(tok)
            if len(r.r) > 24:
                r.r = r.r[-24:]
        for w in writes:
            w.w = tok
            w.r = []

    def op(s, e, fn, reads=(), writes=(), inc=True):
        s._deps(e, reads, writes)
        ins = fn()
        if inc:
            s.cnt[e] += 1
            ins.then_inc(s.csem[e], 1)
            tok = (s.csem[e], s.cnt[e], e)
        else:
            tok = (s.csem[e], s.cnt[e] + 1, e)
        s._commit(tok, reads, writes)
        return ins

    def dma(s, q, fns, reads=(), writes=()):
        d = s.dq[q]
        i = d["i"]
        sem = d["sems"][i]
        if d["vals"][i] > 0:
            s._wait(q, (sem, d["vals"][i], "dma"))
        s._deps(q, reads, writes)
        for fn in fns:
            fn(s.E[q]).then_inc(sem, 16)
            d["vals"][i] += 16
        tok = (sem, d["vals"][i], "dma")
        d["i"] = (i + 1) % s.NQ
        s._commit(tok, reads, writes)
        s.pending.append(tok)
        return tok

    def barrier(s):
        for e in s.E:
            for f in s.csem:
                if f != e and s.cnt[f] > 0:
                    s._wait(e, (s.csem[f], s.cnt[f], f))
            for tok in s.pending:
                s._wait(e, tok)
        s.pending = []


def split_groups(n, mx):
    g = (n + mx - 1) // mx
    base, rem = divmod(n, g)
    out, a = [], 0
    for i in range(g):
        k = base + (1 if i < rem else 0)
        out.append((a, k))
        a += k
    return out


class Prog:
    def __init__(s, cfg, dbg=False, stop_after=None, moe_stop=None):
        s.moe_stop = moe_stop
        s.cfg = cfg
        s.dbg = dbg
        s.stop_after = stop_after
        nc = s.nc = bass.Bass("TRN2", target_bir_lowering=False)
        s.kb = KB(nc)
        D, T, C, TT, FF, E = cfg.D, cfg.T, cfg.C, cfg.TT, cfg.FF, cfg.E

        def inp(name, shape):
            return nc.dram_tensor(name, list(shape), F32, kind="ExternalInput").ap()

        s.x = inp("x", [T, D])
        s.c = inp("c", [1, D])
        s.ctx = inp("ctx", [C, D])
        s.c_ctx = inp("c_ctx", [1, D])
        s.L = []
        for l in range(2):
            p = {}
            p["mod_w"] = inp(f"mod_w_{l}", [D, 6 * D])
            p["mod_b"] = inp(f"mod_b_{l}", [1, 6 * D])
            p["norm1"] = inp(f"norm1_{l}", [1, D])
            p["norm2"] = inp(f"norm2_{l}", [1, D])
            p["wqkv"] = inp(f"wqkv_{l}", [D, 3 * D if l == 0 else D + 2 * cfg.KVD])
            p["wo"] = inp(f"wo_{l}", [D, D])
            p["qn"] = inp(f"qnorm_{l}", [1, 128])
            p["kn"] = inp(f"knorm_{l}", [1, 128])
            if l == 0:
                p["lam_q1"] = inp("lam_q1_0", [1, 128])
                p["lam_k1"] = inp("lam_k1_0", [1, 128])
                p["lam_q2"] = inp("lam_q2_0", [1, 128])
                p["lam_k2"] = inp("lam_k2_0", [1, 128])
                p["subln"] = inp(f"subln_{l}", [1, 256])
            p["router"] = inp(f"router_{l}", [D, E])
            p["wg"] = inp(f"we_gate_{l}", [E, D, FF])
            p["wu"] = inp(f"we_up_{l}", [E, D, FF])
            p["wd"] = inp(f"we_down_{l}", [E, FF, D])
            s.L.append(p)
        s.k_rope_c = inp("k_rope_c", [128, TT])
        s.k_rope_s = inp("k_rope_s", [128, TT])
        s.k_ident = inp("k_ident", [128, 128])
        s.k_rot = inp("k_rot", [128, 128])

        s.out = nc.dram_tensor("out", [T, D], F32, kind="ExternalOutput").ap()

        s.scr = {}

        def scratch(name, shape, dt):
            kind = "ExternalOutput" if dbg else "Internal"
            s.scr[name] = nc.dram_tensor("s_" + name, list(shape), dt, kind=kind).ap()
            return s.scr[name]

        s.modv = scratch("modv", [4, 6 * D], F32)
        s.xres = scratch("xres", [TT, D], F32)
        s.hT = scratch("hT", [128, cfg.KC, TT], BF16)
        s.oT = scratch("oT", [128, cfg.KC, TT], BF16)
        s.qkT = scratch("qkT", [2 * cfg.KC, 128, TT], BF16)
        s.v = scratch("v", [TT, D], BF16)
        s.h2 = scratch("h2", [TT, D], BF16)
        if dbg:
            s.d_aff = scratch("d_aff", [cfg.E, T], F32)
            s.d_idx = scratch("d_idx", [128, D // 512 + 1, cfg.E], U32)
            s.d_gate = scratch("d_gate", [128, cfg.E], F32)

        s.build()

    def build(s):
        kb, nc, cfg = s.kb, s.nc, s.cfg
        with ExitStack() as gst:
            s.consts(gst)
            NBm = 6 * cfg.D // 512
            nfirst = 2 * cfg.D // 512
            for _ in s.adaln_gen([(0, i) for i in range(nfirst)]):
                pass
            kb.barrier()
            bgwork = [(0, i) for i in range(nfirst, NBm)] + [(1, i) for i in range(NBm)]
            phases = []
            for l in range(2):
                phases.append((f"norm1_{l}", lambda l=l: s.phase_norm1(l)))
                phases.append((f"qkv_{l}", lambda l=l: s.phase_qkv(l)))
                phases.append((f"attn_{l}", lambda l=l: s.phase_attn(l)))
                phases.append((f"wo_{l}", lambda l=l: s.phase_wo(l)))
                phases.append((f"moe_{l}", lambda l=l: s.phase_moe(l)))
            for name, fn in phases:
                if name == "attn_0":
                    gen = s.adaln_gen(bgwork, shared_psum=True)
                    state = {"done": False}

                    def step(gen=gen, state=state):
                        if not state["done"]:
                            if next(gen) == "done":
                                state["done"] = True

                    def bg(state=state):
                        state["n"] = state.get("n", 0) + 1
                        if state["n"] % 2 == 0:
                            step()
                    step()
                    s.phase_attn(0, bg=bg)
                    while not state["done"]:
                        step()
                    for _ in gen:
                        pass
                else:
                    fn()
                kb.barrier()
                if s.stop_after == name:
                    break

    def consts(s, st):
        kb, nc = s.kb, s.nc
        s.ident_f = kb.sb(st, "ident_f", [128, 128], F32)
        s.ident_b = kb.sb(st, "ident_b", [128, 128], BF16)
        s.rot_b = kb.sb(st, "rot_b", [128, 128], BF16)
        s.ones_b = kb.sb(st, "ones_b", [128, 128], BF16)
        tmp = kb.sb(st, "ctmp", [128, 128], F32)
        kb.dma("sp", [lambda e: e.dma_start(out=s.ident_f[:, :], in_=s.k_ident[:, :])], writes=[s.ident_f])
        kb.dma("sp", [lambda e: e.dma_start(out=tmp[:, :], in_=s.k_rot[:, :])], writes=[tmp])
        kb.op("dve", lambda: nc.vector.tensor_copy(out=s.ident_b[:, :], in_=s.ident_f[:, :]),
              reads=[s.ident_f], writes=[s.ident_b])
        kb.op("dve", lambda: nc.vector.tensor_copy(out=s.rot_b[:, :], in_=tmp[:, :]),
              reads=[tmp], writes=[s.rot_b])
        kb.op("dve", lambda: nc.vector.memset(s.ones_b[:, :], 1.0), writes=[s.ones_b])
        s.eps_col = kb.sb(st, "eps_col", [128, 1], F32)
        kb.op("dve", lambda: nc.vector.memset(s.eps_col[:, :], NORM_EPS), writes=[s.eps_col])

    def load_w_block(s, q, tile, w_ap, c0, ncols, nk=None):
        kb = s.kb
        src = w_ap[:, c0:c0 + ncols].rearrange("(k p) n -> p k n", p=128)
        KCn = src.shape[1]
        step = max(1, 1024 // 128)
        fns = []
        for k0 in range(0, KCn, step):
            k1 = min(KCn, k0 + step)
            fns.append(lambda e, k0=k0, k1=k1: e.dma_start(out=tile[:, k0:k1, 0:ncols], in_=src[:, k0:k1, :]))
        kb.dma(q, fns, writes=[tile])

    def adaln_gen(s, work, shared_psum=False):
        kb, nc, cfg = s.kb, s.nc, s.cfg
        D, KC = cfg.D, cfg.KC
        with ExitStack() as st:
            craw = kb.sb(st, "craw", [KC, 2, 128], F32)
            csil = kb.sb(st, "csil", [KC, 2, 128], F32)
            sT = kb.sb(st, "sT", [128, KC, 2], BF16)
            pacc = [kb.ps(st, "macc", [128, 512], F32) for _ in range(1 if shared_psum else 2)]
            ps_t = pacc[0]
            kb.dma("sp", [lambda e: e.dma_start(out=craw[:, 0, :], in_=s.c.rearrange("o (k p) -> (o k) p", p=128))],
                   writes=[craw])
            kb.dma("sp", [lambda e: e.dma_start(out=craw[:, 1, :], in_=s.c_ctx.rearrange("o (k p) -> (o k) p", p=128))],
                   writes=[craw])
            kb.op("act", lambda: nc.scalar.activation(out=csil[:, :, :], in_=craw[:, :, :], func=AF.Silu),
                  reads=[craw], writes=[csil])
            for r in range(2):
                kb.op("pe", lambda r=r: nc.tensor.transpose(out=ps_t[:, r * KC:(r + 1) * KC], in_=csil[:, r, :],
                                                              identity=s.ident_f[0:KC, 0:KC]),
                      reads=[csil, s.ident_f], writes=[ps_t])
            for r in range(2):
                kb.op("dve", lambda r=r: nc.vector.tensor_copy(out=sT[:, :, r], in_=ps_t[:, r * KC:(r + 1) * KC]),
                      reads=[ps_t], writes=[sT])
            NBUF = 3
            wb = [kb.sb(st, "mw", [128, KC, 512], BF16) for _ in range(NBUF)]
            bb = [kb.sb(st, "mb", [2, 512], F32) for _ in range(NBUF)]
            ob = [kb.sb(st, "mo", [2, 512], F32) for _ in range(NBUF)]

            def load(n):
                l, i = work[n]
                p = s.L[l]
                s.load_w_block("pool", wb[n % NBUF], p["mod_w"], i * 512, 512)
                kb.dma("pool", [lambda e, r=r: e.dma_start(out=bb[n % NBUF][r:r + 1, :],
                                                           in_=p["mod_b"][0:1, i * 512:(i + 1) * 512])
                              for r in range(2)], writes=[bb[n % NBUF]])

            if work:
                load(0)
            for n in range(len(work)):
                l, i = work[n]
                if n + 1 < len(work):
                    load(n + 1)
                w, acc, o, b = wb[n % NBUF], pacc[n % len(pacc)], ob[n % NBUF], bb[n % NBUF]
                for kc in range(KC):
                    kb.op("pe", lambda kc=kc: nc.tensor.matmul(acc[0:2, :], lhsT=sT[:, kc, :], rhs=w[:, kc, :],
                                                               start=(kc == 0), stop=(kc == KC - 1)),
                          reads=[sT, w], writes=[acc], inc=(kc == KC - 1))
                kb.op("dve", lambda: nc.vector.tensor_tensor(out=o[:, :], in0=acc[0:2, :], in1=b[:, :], op=ALU.add),
                      reads=[acc, b], writes=[o])
                kb.dma("pool", [lambda e: e.dma_start(out=s.modv[2 * l:2 * l + 2, i * 512:(i + 1) * 512], in_=o[:, :])],
                       reads=[o])
                yield n
            yield "done"

    def bc_load(s, tile, src_row_ap):
        s.kb.dma("sp", [lambda e: e.dma_start(out=tile[:, :], in_=src_row_ap.partition_broadcast(128))],
                 writes=[tile])

    def mod_row(s, l, r, j):
        D = s.cfg.D
        return s.modv[2 * l + r:2 * l + r + 1, j * D:(j + 1) * D]

    def x_src(s, l, which, ti):
        cfg = s.cfg
        lat = ti < cfg.NTL
        if l == 0 and which == "in":
            return s.x[ti * 128:(ti + 1) * 128, :] if lat else s.ctx[(ti - cfg.NTL) * 128:(ti - cfg.NTL + 1) * 128, :]
        if l == 1 and which == "mid":
            assert lat
            return s.out[ti * 128:(ti + 1) * 128, :]
        return s.xres[ti * 128:(ti + 1) * 128, :]

    def x_dst(s, l, ti):
        if l == 1:
            return s.out[ti * 128:(ti + 1) * 128, :]
        return s.xres[ti * 128:(ti + 1) * 128, :]

    def norm_setup(s, st, l, which, stg=None):
        kb, nc, cfg = s.kb, s.nc, s.cfg
        D = cfg.D
        p = s.L[l]
        jsh, jsc = (0, 1) if which == 1 else (3, 4)
        AB = []
        nr = 2 if (l == 0 or which == 1) else 1
        for r in range(nr):
            AB.append((kb.sb(st, "nA", [128, D], F32), kb.sb(st, "nB", [128, D], F32)))
        gain = kb.sb(stg if stg is not None else st, "gain", [128, D], F32)
        s.bc_load(gain, p["norm1" if which == 1 else "norm2"][0:1, :])
        for r, (A, B) in enumerate(AB):
            s.bc_load(A, s.mod_row(l, r, jsc))
            s.bc_load(B, s.mod_row(l, r, jsh))
            kb.op("dve", lambda A=A: nc.vector.scalar_tensor_tensor(out=A[:, :], in0=A[:, :], scalar=1.0, in1=gain[:, :],
                                                                    op0=ALU.add, op1=ALU.mult),
                  reads=[gain, A], writes=[A])
        return AB

    def rsqrt_col(s, out, in_, np_, scale, eps):
        kb, nc = s.kb, s.nc
        kb.op("act", lambda: nc.scalar.activation(out=out[0:np_, 0:1], in_=in_[0:np_, 0:1], func=AF.Sqrt,
                                                  scale=scale, bias=s.eps_col[0:np_, 0:1] if eps == NORM_EPS else eps),
              reads=[in_, s.eps_col], writes=[out])
        kb.op("dve", lambda: nc.vector.reciprocal(out=out[0:np_, 0:1], in_=out[0:np_, 0:1]), reads=[out], writes=[out])

    def norm_tile(s, xt, A, B, ssq, rstd, junk, y_out, h_out):
        kb, nc, D = s.kb, s.nc, s.cfg.D
        kb.op("act", lambda: nc.scalar.activation(out=junk[:, :], in_=xt[:, :], func=AF.Square, accum_out=ssq[:, 0:1]),
              reads=[xt], writes=[junk, ssq])
        s.rsqrt_col(rstd, ssq, 128, 1.0 / D, NORM_EPS)
        kb.op("dve", lambda: nc.vector.scalar_tensor_tensor(out=y_out[:, :], in0=xt[:, :], scalar=rstd[:, 0:1], in1=A[:, :],
                                                            op0=ALU.mult, op1=ALU.mult),
              reads=[xt, rstd, A], writes=[y_out])
        kb.op("dve", lambda: nc.vector.tensor_tensor(out=h_out[:, :], in0=y_out[:, :], in1=B[:, :], op=ALU.add),
              reads=[y_out, B], writes=[h_out])

    def phase_norm1(s, l):
        kb, nc, cfg = s.kb, s.nc, s.cfg
        D, KC = cfg.D, cfg.KC
        with ExitStack() as st:
            AB = s.norm_setup(st, l, 1)
            xt = [kb.sb(st, "xt", [128, D], F32) for _ in range(2)]
            hb = [kb.sb(st, "hb", [128, D], BF16) for _ in range(2)]
            junk = kb.sb(st, "junk", [128, D], BF16)
            ssq = kb.sb(st, "ssq", [128, 1], F32)
            rstd = kb.sb(st, "rstd", [128, 1], F32)
            hst = [kb.sb(st, "hst", [128, KC, 128], BF16) for _ in range(2)]
            NPB = min(8, KC)
            pst = [kb.ps(st, "pst", [128, 8, 128], BF16) for _ in range(4)]
            pi = [0]

            def stage1(ti):
                A, B = AB[0 if ti < cfg.NTL else 1]
                x_t, h_t = xt[ti % 2], hb[ti % 2]
                kb.dma("sp", [lambda e: e.dma_start(out=x_t[:, :], in_=s.x_src(l, "in", ti))], writes=[x_t])
                s.norm_tile(x_t, A, B, ssq, rstd, junk, x_t, h_t)

            def stage2(ti):
                h_t, stg = hb[ti % 2], hst[ti % 2]
                for k0 in range(0, KC, NPB):
                    pt = pst[pi[0] % 4]
                    pi[0] += 1
                    for j in range(NPB):
                        kb.op("pe", lambda j=j: nc.tensor.transpose(out=pt[:, j, :], in_=h_t[:, (k0 + j) * 128:(k0 + j + 1) * 128],
                                                                    identity=s.ident_b[:, :]),
                              reads=[h_t, s.ident_b], writes=[pt], inc=(j == NPB - 1))
                    kb.op("act", lambda: nc.scalar.copy(out=stg[:, k0:k0 + NPB, :], in_=pt[:, 0:NPB, :]),
                          reads=[pt], writes=[stg])
                step = min(8, KC)
                kb.dma("sp", [lambda e, k0=k0: e.dma_start(out=s.hT[:, k0:k0 + step, ti * 128:(ti + 1) * 128],
                                                            in_=stg[:, k0:k0 + step, :])
                              for k0 in range(0, KC, step)], reads=[stg])

            stage1(0)
            for ti in range(cfg.NT):
                if ti + 1 < cfg.NT:
                    stage1(ti + 1)
                stage2(ti)

    def col_load(s, tile, row_ap, n):
        s.kb.dma("sp", [lambda e: e.dma_start(out=tile[0:n, 0:1], in_=row_ap.rearrange("o p -> p o"))], writes=[tile])

    def phase_qkv(s, l):
        kb, nc, cfg = s.kb, s.nc, s.cfg
        D, KC, TT, NT = cfg.D, cfg.KC, cfg.TT, cfg.NT
        p = s.L[l]
        nq = D // 128
        nfm = (2 * D) // 128 if l == 0 else (D + cfg.KVD) // 128
        vc0 = 2 * D if l == 0 else D + cfg.KVD
        vcols = D if l == 0 else cfg.KVD
        vb = min(256, vcols)
        with ExitStack() as st:
            gq = kb.sb(st, "gq", [128, 1], F32)
            gk = kb.sb(st, "gk", [128, 1], F32)
            s.col_load(gq, p["qn"], 128)
            s.col_load(gk, p["kn"], 128)
            cosT = kb.sb(st, "cosT", [128, TT], F32)
            sinT = kb.sb(st, "sinT", [128, TT], F32)
            kb.dma("sp", [lambda e: e.dma_start(out=cosT[:, :], in_=s.k_rope_c[:, :])], writes=[cosT])
            kb.dma("sp", [lambda e: e.dma_start(out=sinT[:, :], in_=s.k_rope_s[:, :])], writes=[sinT])
            TB = split_groups(NT, 9)
            mxnt = max(k for _, k in TB)
            hT_sb = kb.sb(st, "hT_sb", [128, KC, mxnt * 128], BF16)
            wfm = [kb.sb(st, "wfm", [128, KC, 128], BF16) for _ in range(3)]
            wv = [kb.sb(st, "wv", [128, KC, vb], BF16) for _ in range(2)]
            qst = [kb.sb(st, "qst", [128, mxnt * 128], BF16) for _ in range(2)]
            sqb = [kb.sb(st, "sqb", [128, 512], BF16) for _ in range(2)]
            rs = [kb.sb(st, "rs", [128, 512], F32) for _ in range(2)]
            xb = [kb.sb(st, "xb", [128, 512], BF16) for _ in range(2)]
            t1 = [kb.sb(st, "t1", [128, 512], F32) for _ in range(2)]
            t2 = [kb.sb(st, "t2", [128, 512], F32) for _ in range(2)]
            vst = [kb.sb(st, "vst", [128, vb], BF16) for _ in range(3)]
            pacc = [kb.ps(st, "pacc", [128, 512], F32) for _ in range(3)]
            pss = [kb.ps(st, "pss", [128, 512], F32) for _ in range(2)]
            psr = [kb.ps(st, "psr", [128, 512], F32) for _ in range(2)]
            pv = pacc
            for (t0, nt) in TB:
                kstep = min(8, KC)
                for k0 in range(0, KC, kstep):
                    kb.dma("sp", [lambda e, k0=k0: e.dma_start(out=hT_sb[:, k0:k0 + kstep, 0:nt * 128],
                                                                in_=s.hT[:, k0:k0 + kstep, t0 * 128:(t0 + nt) * 128])],
                           writes=[hT_sb])
                groups = split_groups(nt, 4)
                items = [(j, gidx, a, k) for j in range(nfm) for gidx, (a, k) in enumerate(groups)]
                s.load_w_block("pool", wfm[0], p["wqkv"], 0, 128)

                def M(i):
                    j, gidx, a, k = items[i]
                    if gidx == 0 and j + 1 < nfm:
                        s.load_w_block("pool", wfm[(j + 1) % 3], p["wqkv"], (j + 1) * 128, 128)
                    W, acc, n, c0 = wfm[j % 3], pacc[i % 3], k * 128, a * 128
                    for kc in range(KC):
                        kb.op("pe", lambda kc=kc: nc.tensor.matmul(acc[:, 0:n], lhsT=W[:, kc, :], rhs=hT_sb[:, kc, c0:c0 + n],
                                                                   start=(kc == 0), stop=(kc == KC - 1)),
                              reads=[W, hT_sb], writes=[acc], inc=(kc == KC - 1))

                def E1(i):
                    j, gidx, a, k = items[i]
                    acc, n, sq_, ss = pacc[i % 3], k * 128, sqb[i % 2], pss[i % 2]
                    kb.op("act", lambda: nc.scalar.activation(out=sq_[:, 0:n], in_=acc[:, 0:n], func=AF.Square),
                          reads=[acc], writes=[sq_])
                    kb.op("pe", lambda: nc.tensor.matmul(ss[:, 0:n], lhsT=s.ones_b[:, :], rhs=sq_[:, 0:n], start=True, stop=True),
                          reads=[s.ones_b, sq_], writes=[ss])

                def E2(i):
                    j, gidx, a, k = items[i]
                    acc, n, ss, rs_, xb_, sr = pacc[i % 3], k * 128, pss[i % 2], rs[i % 2], xb[i % 2], psr[i % 2]
                    gain = gq if j < nq else gk
                    kb.op("act", lambda: nc.scalar.activation(out=rs_[:, 0:n], in_=ss[:, 0:n], func=AF.Sqrt,
                                                              scale=1.0 / 128, bias=s.eps_col[:, 0:1]),
                          reads=[ss, s.eps_col], writes=[rs_])
                    kb.op("dve", lambda: nc.vector.reciprocal(out=rs_[:, 0:n], in_=rs_[:, 0:n]), reads=[rs_], writes=[rs_])
                    kb.op("dve", lambda: nc.vector.scalar_tensor_tensor(out=xb_[:, 0:n], in0=acc[:, 0:n], scalar=gain[:, 0:1],
                                                                        in1=rs_[:, 0:n], op0=ALU.mult, op1=ALU.mult),
                          reads=[acc, gain, rs_], writes=[xb_])
                    kb.op("pe", lambda: nc.tensor.matmul(sr[:, 0:n], lhsT=s.rot_b[:, :], rhs=xb_[:, 0:n], start=True, stop=True),
                          reads=[s.rot_b, xb_], writes=[sr])

                def E3(i):
                    j, gidx, a, k = items[i]
                    n, c0, tok0 = k * 128, a * 128, (t0 + a) * 128
                    xb_, sr, t1_, t2_, qs = xb[i % 2], psr[i % 2], t1[i % 2], t2[i % 2], qst[j % 2]
                    kb.op("dve", lambda: nc.vector.tensor_tensor(out=t1_[:, 0:n], in0=xb_[:, 0:n], in1=cosT[:, tok0:tok0 + n],
                                                                 op=ALU.mult), reads=[xb_, cosT], writes=[t1_])
                    kb.op("dve", lambda: nc.vector.tensor_tensor(out=t2_[:, 0:n], in0=sr[:, 0:n], in1=sinT[:, tok0:tok0 + n],
                                                                 op=ALU.mult), reads=[sr, sinT], writes=[t2_])
                    kb.op("dve", lambda: nc.vector.tensor_tensor(out=qs[:, c0:c0 + n], in0=t1_[:, 0:n], in1=t2_[:, 0:n],
                                                                 op=ALU.add), reads=[t1_, t2_], writes=[qs])
                    if gidx == len(groups) - 1:
                        kb.dma("sp", [lambda e: e.dma_start(out=s.qkT[j, :, t0 * 128:(t0 + nt) * 128], in_=qs[:, 0:nt * 128])],
                               reads=[qs])

                ni = len(items)
                for i in range(ni + 3):
                    if i < ni:
                        M(i)
                    if 0 <= i - 1 < ni:
                        E1(i - 1)
                    if 0 <= i - 2 < ni:
                        E2(i - 2)
                    if 0 <= i - 3 < ni:
                        E3(i - 3)
                nvb = vcols // vb
                s.load_w_block("pool", wv[0], p["wqkv"], vc0, vb)
                vi = 0
                for jv in range(nvb):
                    if jv + 1 < nvb:
                        s.load_w_block("pool", wv[(jv + 1) % 2], p["wqkv"], vc0 + (jv + 1) * vb, vb)
                    W = wv[jv % 2]
                    for ti in range(nt):
                        acc = pv[vi % 2]
                        vs = vst[vi % 3]
                        vi += 1
                        for kc in range(KC):
                            kb.op("pe", lambda kc=kc: nc.tensor.matmul(acc[:, 0:vb], lhsT=hT_sb[:, kc, ti * 128:(ti + 1) * 128],
                                                                       rhs=W[:, kc, :], start=(kc == 0), stop=(kc == KC - 1)),
                                  reads=[W, hT_sb], writes=[acc], inc=(kc == KC - 1))
                        kb.op("act", lambda: nc.scalar.copy(out=vs[:, :], in_=acc[:, 0:vb]), reads=[acc], writes=[vs])
                        kb.dma("sp", [lambda e: e.dma_start(out=s.v[(t0 + ti) * 128:(t0 + ti + 1) * 128, jv * vb:(jv + 1) * vb],
                                                            in_=vs[:, :])], reads=[vs])

    def phase_attn(s, l, bg=None):
        kb, nc, cfg = s.kb, s.nc, s.cfg
        D, KC, TT, NT, NTL, NTC = cfg.D, cfg.KC, cfg.TT, cfg.NT, cfg.NTL, cfg.NTC
        p = s.L[l]
        nq = D // 128
        dv = 256 if l == 0 else 128
        scale = 1.0 / math.sqrt(128.0)
        lam_init = 0.8 - 0.6 * math.exp(-0.3 * l)
        with ExitStack() as st:
            gqb = kb.sb(st, "gqb", [128, 128], F32)
            gkb = kb.sb(st, "gkb", [128, 128], F32)
            s.bc_load(gqb, p["qn"][0:1, :])
            s.bc_load(gkb, p["kn"][0:1, :])
            mq = kb.sb(st, "mq", [128, 1], F32)
            mk = kb.sb(st, "mk", [128, 1], F32)
            negC = kb.sb(st, "negC", [128, 1], F32)
            for (gb, m) in ((gqb, mq), (gkb, mk)):
                kb.op("act", lambda gb=gb: nc.scalar.activation(out=gb[:, :], in_=gb[:, :], func=AF.Abs), reads=[gb], writes=[gb])
                kb.op("dve", lambda gb=gb, m=m: nc.vector.tensor_reduce(out=m[:, 0:1], in_=gb[:, :], axis=AX.X, op=ALU.max),
                      reads=[gb], writes=[m])
            kb.op("dve", lambda: nc.vector.scalar_tensor_tensor(out=negC[:, 0:1], in0=mq[:, 0:1], scalar=-math.sqrt(128.0) * scale * 1.0,
                                                                in1=mk[:, 0:1], op0=ALU.mult, op1=ALU.mult),
                  reads=[mq, mk], writes=[negC])
            kb.op("dve", lambda: nc.vector.tensor_scalar(out=negC[:, 0:1], in0=negC[:, 0:1], scalar1=math.sqrt(128.0), scalar2=None,
                                                         op0=ALU.mult), reads=[negC], writes=[negC])
            if l == 0:
                lt = [kb.sb(st, "lt", [128, 128], F32) for _ in range(4)]
                for i, nm in enumerate(("lam_q1", "lam_k1", "lam_q2", "lam_k2")):
                    s.bc_load(lt[i], p[nm][0:1, :])
                d1 = kb.sb(st, "d1", [128, 1], F32)
                d2 = kb.sb(st, "d2", [128, 1], F32)
                neglam = kb.sb(st, "neglam", [128, 1], F32)
                for (a, b, d) in ((lt[0], lt[1], d1), (lt[2], lt[3], d2)):
                    kb.op("dve", lambda a=a, b=b: nc.vector.tensor_tensor(out=a[:, :], in0=a[:, :], in1=b[:, :], op=ALU.mult),
                          reads=[a, b], writes=[a])
                    kb.op("dve", lambda a=a, d=d: nc.vector.tensor_reduce(out=d[:, 0:1], in_=a[:, :], axis=AX.X, op=ALU.add),
                          reads=[a], writes=[d])
                    kb.op("act", lambda d=d: nc.scalar.activation(out=d[:, 0:1], in_=d[:, 0:1], func=AF.Exp), reads=[d], writes=[d])
                kb.op("dve", lambda: nc.vector.scalar_tensor_tensor(out=neglam[:, 0:1], in0=d2[:, 0:1], scalar=-lam_init, in1=d1[:, 0:1],
                                                                    op0=ALU.add, op1=ALU.subtract), reads=[d1, d2], writes=[neglam])
                subw = kb.sb(st, "subw", [128, 256], F32)
                s.bc_load(subw, p["subln"][0:1, :])
                kb.op("dve", lambda: nc.vector.tensor_scalar(out=subw[:, :], in0=subw[:, :], scalar1=1.0 - lam_init, scalar2=None,
                                                             op0=ALU.mult), reads=[subw], writes=[subw])
                o1 = kb.sb(st, "o1", [128, 4, 256], F32)
                od = [kb.sb(st, "od", [128, 256], F32) for _ in range(2)]
                junk = kb.sb(st, "ajunk", [128, 256], BF16)
                ssq = kb.sb(st, "assq", [128, 1], F32)
                rstd = kb.sb(st, "arstd", [128, 1], F32)
            nkb = 2 if l == 0 else 1
            kTs = [kb.sb(st, "kT", [128, TT], BF16) for _ in range(2 * nkb)]
            Vaug = [kb.sb(st, "Vaug", [128, NT, dv + 1], BF16) for _ in range(2)]
            for V in Vaug:
                kb.op("dve", lambda V=V: nc.vector.memset(V[:, :, :], 1.0), writes=[V])
            qTs = [kb.sb(st, "qT", [128, 512], BF16) for _ in range(3)]
            PTs = [kb.sb(st, "PT", [128, 512], BF16) for _ in range(3)]
            rc = [kb.sb(st, "rc", [128, 1], F32) for _ in range(4)]
            ob = [kb.sb(st, "ob", [128, dv], BF16) for _ in range(4)]
            nch = dv // 128
            ostage = [kb.sb(st, "ostage", [128, nch, 512], BF16) for _ in range(3)]
            pS = [kb.ps(st, "pS", [128, 512], F32) for _ in range(2)]
            pO = [kb.ps(st, "pO", [128, 512], F32) for _ in range(4)]
            pT = [kb.ps(st, "pT", [128, 8, 128], BF16) for _ in range(1 if bg is not None else 2)]
            if l == 0:
                odq = [kb.sb(st, "odq", [128, 256], F32) for _ in range(4)]
                sq = kb.sb(st, "asq", [128, 256], F32)
                ssqq = [kb.sb(st, "assq", [128, 1], F32) for _ in range(4)]
                rstq = [kb.sb(st, "arst", [128, 1], F32) for _ in range(4)]

            qgroups = [(a, k, list(range(NT))) for (a, k) in split_groups(NTL, 4)]
            if l == 0:
                qgroups.append((NTL, NTC, list(range(NTL, NT))))
            ngrp = cfg.H0 if l == 0 else cfg.KV1
            cnt = {"q": 0, "pt": 0, "os": 0, "pT": 0}
            pending = []

            def flush(upto=None):
                while pending and (upto is None or pending[0][0] <= upto):
                    pending.pop(0)[1]()

            for g in range(ngrp):
                kset = kTs[(g % 2) * nkb:(g % 2) * nkb + nkb]
                V = Vaug[g % 2]
                if l == 0:
                    kblocks = [nq + 2 * g, nq + 2 * g + 1]
                    units = [(2 * g, 0), (2 * g + 1, 1)]
                    vcol0 = g * 256
                else:
                    kblocks = [nq + g]
                    units = [(4 * g + i, 0) for i in range(4)]
                    vcol0 = g * 128
                for i, kbk in enumerate(kblocks):
                    kb.dma("sp", [lambda e, i=i, kbk=kbk: e.dma_start(out=kset[i][:, :], in_=s.qkT[kbk, :, :])], writes=[kset[i]])
                kb.dma("sp", [lambda e: e.dma_start(out=V[:, :, 0:dv],
                                                    in_=s.v[:, vcol0:vcol0 + dv].rearrange("(t p) e -> p t e", p=128))], writes=[V])
                for (qt0, nqt, keyt) in qgroups:
                    n = nqt * 128
                    for ui, (qblk, kidx) in enumerate(units):
                        qT = qTs[cnt["q"] % 3]
                        cnt["q"] += 1
                        kb.dma("sp", [lambda e: e.dma_start(out=qT[:, 0:n], in_=s.qkT[qblk, :, qt0 * 128:qt0 * 128 + n])], writes=[qT])
                        kT = kset[kidx]
                        nk = len(keyt)

                        def emit_S(ki):
                            kt = keyt[ki]
                            ps_ = pS[ki % 2]
                            kb.op("pe", lambda: nc.tensor.matmul(ps_[:, 0:n], lhsT=kT[:, kt * 128:(kt + 1) * 128], rhs=qT[:, 0:n],
                                                                 start=True, stop=True), reads=[kT, qT], writes=[ps_])
                        emit_S(0)
                        for ki in range(nk):
                            kt = keyt[ki]
                            if ki + 1 < nk:
                                emit_S(ki + 1)
                            ps_ = pS[ki % 2]
                            PT = PTs[cnt["pt"] % 3]
                            cnt["pt"] += 1
                            kb.op("act", lambda: nc.scalar.activation(out=PT[:, 0:n], in_=ps_[:, 0:n], func=AF.Exp, scale=scale,
                                                                      bias=negC[:, 0:1]), reads=[ps_, negC], writes=[PT])
                            for qi in range(nqt):
                                kb.op("pe", lambda qi=qi: nc.tensor.matmul(pO[qi][:, 0:dv + 1], lhsT=PT[:, qi * 128:(qi + 1) * 128], rhs=V[:, kt, :],
                                                                          start=(ki == 0), stop=(ki == nk - 1)),
                                      reads=[PT, V], writes=[pO[qi]], inc=(ki == nk - 1 or qi == nqt - 1))
                            if ki >= 2 and (ki - 2) % 3 == 0:
                                flush(upto=(ki - 2) // 3)
                        flush()
                        if bg is not None:
                            bg()
                        has_out = (l == 1) or (ui == 1)
                        if has_out:
                            ost = ostage[cnt["os"] % 3]
                            cnt["os"] += 1
                        for qi in range(nqt):
                            O = pO[qi]
                            r_ = rc[qi]
                            kb.op("dve", lambda: nc.vector.reciprocal(out=r_[:, 0:1], in_=O[:, dv:dv + 1]), reads=[O], writes=[r_])
                            if l == 1:
                                o_b = ob[qi]
                                kb.op("dve", lambda: nc.vector.tensor_scalar(out=o_b[:, :], in0=O[:, 0:dv], scalar1=r_[:, 0:1], scalar2=None,
                                                                             op0=ALU.mult), reads=[O, r_], writes=[o_b])
                            elif ui == 0:
                                kb.op("dve", lambda: nc.vector.tensor_scalar(out=o1[:, qi, :], in0=O[:, 0:dv], scalar1=r_[:, 0:1], scalar2=None,
                                                                             op0=ALU.mult), reads=[O, r_], writes=[o1])
                            else:
                                od_ = odq[qi]
                                kb.op("dve", lambda: nc.vector.tensor_tensor(out=r_[:, 0:1], in0=r_[:, 0:1], in1=neglam[:, 0:1], op=ALU.mult),
                                      reads=[r_, neglam], writes=[r_])
                                kb.op("dve", lambda: nc.vector.scalar_tensor_tensor(out=od_[:, :], in0=O[:, 0:dv], scalar=r_[:, 0:1],
                                                                                    in1=o1[:, qi, :], op0=ALU.mult, op1=ALU.add),
                                      reads=[O, r_, o1], writes=[od_])
                                kb.op("dve", lambda: nc.vector.tensor_tensor(out=sq[:, :], in0=od_[:, :], in1=od_[:, :], op=ALU.mult),
                                      reads=[od_], writes=[sq])
                                kb.op("dve", lambda: nc.vector.tensor_reduce(out=ssqq[qi][:, 0:1], in_=sq[:, :], axis=AX.X, op=ALU.add),
                                      reads=[sq], writes=[ssqq[qi]])
                        if not has_out:
                            continue

                        def make_Eb(qi, ost=ost, qblk=qblk, g=g, qt0=qt0, n=n, nqt=nqt):
                            def fn():
                                o_b = ob[qi]
                                if l == 0:
                                    kb.op("act", lambda: nc.scalar.activation(out=rstq[qi][:, 0:1], in_=ssqq[qi][:, 0:1], func=AF.Ln,
                                                                              scale=1.0 / 256, bias=s.eps_col[:, 0:1]),
                                          reads=[ssqq[qi], s.eps_col], writes=[rstq[qi]])
                                    kb.op("act", lambda: nc.scalar.activation(out=rstq[qi][:, 0:1], in_=rstq[qi][:, 0:1], func=AF.Exp, scale=-0.5),
                                          reads=[rstq[qi]], writes=[rstq[qi]])
                                    kb.op("dve", lambda: nc.vector.scalar_tensor_tensor(out=o_b[:, :], in0=odq[qi][:, :], scalar=rstq[qi][:, 0:1],
                                                                                        in1=subw[:, :], op0=ALU.mult, op1=ALU.mult),
                                          reads=[odq[qi], rstq[qi], subw], writes=[o_b])
                                pt_ = pT[cnt["pT"] % len(pT)]
                                cnt["pT"] += 1
                                for c in range(nch):
                                    kb.op("pe", lambda c=c: nc.tensor.transpose(out=pt_[:, c, :], in_=o_b[:, c * 128:(c + 1) * 128],
                                                                                identity=s.ident_b[:, :]),
                                          reads=[o_b, s.ident_b], writes=[pt_], inc=(c == nch - 1))
                                kb.op("dve", lambda: nc.vector.tensor_copy(out=ost[:, :, qi * 128:(qi + 1) * 128], in_=pt_[:, 0:nch, :]),
                                      reads=[pt_], writes=[ost])
                                if qi == nqt - 1:
                                    if l == 1:
                                        kb.dma("sp", [lambda e: e.dma_start(out=s.oT[:, qblk, qt0 * 128:qt0 * 128 + n], in_=ost[:, 0, 0:n])],
                                               reads=[ost])
                                    else:
                                        kb.dma("sp", [lambda e: e.dma_start(out=s.oT[:, 2 * g:2 * g + 2, qt0 * 128:qt0 * 128 + n],
                                                                            in_=ost[:, :, 0:n])], reads=[ost])
                            return fn
                        for qi in range(nqt):
                            pending.append((qi, make_Eb(qi)))
            flush()

    def phase_wo(s, l):
        kb, nc, cfg = s.kb, s.nc, s.cfg
        D, KC, NT, NTL = cfg.D, cfg.KC, cfg.NT, cfg.NTL
        p = s.L[l]
        ntiles = NT if l == 0 else NTL
        TB = split_groups(ntiles, 9)
        mxnt = max(k for _, k in TB)
        with ExitStack() as st:
            oT_sb = kb.sb(st, "oT_sb", [128, KC, mxnt * 128], BF16)
            wb = [kb.sb(st, "wob", [128, KC, 512], BF16) for _ in range(2)]
            g1 = [[kb.sb(st, "g1", [128, 512], F32) for _ in range(2)] for _ in range(2)]
            xt = [kb.sb(st, "wxt", [128, 512], F32) for _ in range(3)]
            tmp = [kb.sb(st, "wtmp", [128, 512], F32) for _ in range(2)]
            xo = [kb.sb(st, "wxo", [128, 512], F32) for _ in range(3)]
            pacc = [kb.ps(st, "wacc", [128, 512], F32) for _ in range(2)]
            NB = D // 512
            it = 0
            wi = 0
            for (t0, nt) in TB:
                kstep = min(8, KC)
                for k0 in range(0, KC, kstep):
                    kb.dma("sp", [lambda e, k0=k0: e.dma_start(out=oT_sb[:, k0:k0 + kstep, 0:nt * 128],
                                                                in_=s.oT[:, k0:k0 + kstep, t0 * 128:(t0 + nt) * 128])],
                           writes=[oT_sb])
                s.load_w_block("pool", wb[wi % 2], p["wo"], 0, 512)
                for nb in range(NB):
                    W = wb[wi % 2]
                    wi += 1
                    if nb + 1 < NB:
                        s.load_w_block("pool", wb[wi % 2], p["wo"], (nb + 1) * 512, 512)
                    gl = g1[nb % 2]
                    nr = 2 if l == 0 else 1
                    for r in range(nr):
                        s.bc_load(gl[r], s.mod_row(l, r, 2)[0:1, nb * 512:(nb + 1) * 512])
                    for ti in range(nt):
                        gt = t0 + ti
                        gg = gl[0 if gt < NTL else 1]
                        x_, tm, xo_, acc = xt[it % 3], tmp[it % 2], xo[it % 3], pacc[it % 2]
                        it += 1
                        kb.dma("sp", [lambda e: e.dma_start(out=x_[:, :], in_=s.x_src(l, "in", gt)[:, nb * 512:(nb + 1) * 512])], writes=[x_])
                        for kc in range(KC):
                            kb.op("pe", lambda kc=kc: nc.tensor.matmul(acc[:, :], lhsT=oT_sb[:, kc, ti * 128:(ti + 1) * 128], rhs=W[:, kc, :],
                                                                       start=(kc == 0), stop=(kc == KC - 1)),
                                  reads=[oT_sb, W], writes=[acc], inc=(kc == KC - 1))
                        kb.op("dve", lambda: nc.vector.tensor_tensor(out=tm[:, :], in0=acc[:, :], in1=gg[:, :], op=ALU.mult),
                              reads=[acc, gg], writes=[tm])
                        kb.op("dve", lambda: nc.vector.tensor_tensor(out=xo_[:, :], in0=tm[:, :], in1=x_[:, :], op=ALU.add),
                              reads=[tm, x_], writes=[xo_])
                        kb.dma("sp", [lambda e: e.dma_start(out=s.x_dst(l, gt)[:, nb * 512:(nb + 1) * 512], in_=xo_[:, :])], reads=[xo_])

    def phase_moe(s, l):
        kb, nc, cfg = s.kb, s.nc, s.cfg
        D, KC, NT, NTL, T, C, E, FF, FC = cfg.D, cfg.KC, cfg.NT, cfg.NTL, cfg.T, cfg.C, cfg.E, cfg.FF, cfg.FC
        p = s.L[l]
        has_ctx = (l == 0)
        ntiles = NT if has_ctx else NTL
        NB = D // 512
        CAPL, CAPC = cfg.CAPL, cfg.CAPC
        groups = [(a, min(128, CAPL), 0) for a in range(0, CAPL, 128)]
        if has_ctx:
            groups.append((0, CAPC, 1))
        S = sum(g[1] for g in groups)
        tgt = (s.xres if l == 0 else s.out).rearrange("t (b c) -> (t b) c", c=512)
        with ExitStack() as st0:
            idxT = [kb.sb(st0, "idxT", [128, NB + 1, E], U32) for _ in groups]
            gateT = [kb.sb(st0, "gateT", [128, E], F32) for _ in groups]
            st_aff = ExitStack()
            affT = [kb.sb(st_aff, "affT", [E, T], F32)]
            if has_ctx:
                affT.append(kb.sb(st_aff, "affTc", [E, C], F32))
            with ExitStack() as st:
                with ExitStack() as stg:
                    AB = s.norm_setup(st, l, 2, stg)
                    kb.barrier()
                rt = kb.sb(st, "router", [128, KC, E], F32)
                kstep = min(8, KC)
                kb.dma("sp", [lambda e, k0=k0: e.dma_start(out=rt[:, k0:k0 + kstep, :],
                                                            in_=p["router"].rearrange("(k p) e -> p k e", p=128)[:, k0:k0 + kstep, :])
                              for k0 in range(0, KC, kstep)], writes=[rt])
                xt = [kb.sb(st, "xt2", [128, D], F32) for _ in range(2)]
                hb = [kb.sb(st, "hb2", [128, D], BF16) for _ in range(2)]
                junk = kb.sb(st, "junk2", [128, D], BF16)
                ssq = kb.sb(st, "ssq2", [128, 1], F32)
                rstd = kb.sb(st, "rstd2", [128, 1], F32)
                h2T = kb.sb(st, "h2T", [128, KC, 128], F32)
                mx = kb.sb(st, "mx", [128, 1], F32)
                se = kb.sb(st, "se", [128, 1], F32)
                ex = kb.sb(st, "ex", [128, E], F32)
                aff = kb.sb(st, "aff", [128, E], F32)
                ptf = [kb.ps(st, "ptf", [128, 4, 128], F32) for _ in range(3)]
                plg = kb.ps(st, "plg", [128, 512], F32)
                pat = kb.ps(st, "pat", [128, 512], F32)
                pi = [0]

                def stage1(ti):
                    lat = ti < NTL
                    A, B = AB[0 if lat else 1]
                    x_t, h_b = xt[ti % 2], hb[ti % 2]
                    kb.dma("sp", [lambda e: e.dma_start(out=x_t[:, :], in_=s.x_src(l, "mid", ti))], writes=[x_t])
                    s.norm_tile(x_t, A, B, ssq, rstd, junk, x_t, x_t)
                    kb.op("act", lambda: nc.scalar.copy(out=h_b[:, :], in_=x_t[:, :]), reads=[x_t], writes=[h_b])
                    kb.dma("sp", [lambda e: e.dma_start(out=s.h2[ti * 128:(ti + 1) * 128, :], in_=h_b[:, :])], reads=[h_b])

                def stage2(ti):
                    lat = ti < NTL
                    x_t = xt[ti % 2]
                    NPB = min(4, KC)
                    for k0 in range(0, KC, NPB):
                        pt = ptf[pi[0] % 3]
                        pi[0] += 1
                        for j in range(NPB):
                            kb.op("pe", lambda j=j: nc.tensor.transpose(out=pt[:, j, :], in_=x_t[:, (k0 + j) * 128:(k0 + j + 1) * 128],
                                                                        identity=s.ident_f[:, :]),
                                  reads=[x_t, s.ident_f], writes=[pt], inc=(j == NPB - 1))
                        kb.op("dve", lambda: nc.vector.tensor_copy(out=h2T[:, k0:k0 + NPB, :], in_=pt[:, 0:NPB, :]), reads=[pt], writes=[h2T])
                    for kc in range(KC):
                        kb.op("pe", lambda kc=kc: nc.tensor.matmul(plg[:, 0:E], lhsT=h2T[:, kc, :], rhs=rt[:, kc, :],
                                                                   start=(kc == 0), stop=(kc == KC - 1)),
                              reads=[h2T, rt], writes=[plg], inc=(kc == KC - 1))
                    kb.op("dve", lambda: nc.vector.tensor_reduce(out=mx[:, 0:1], in_=plg[:, 0:E], axis=AX.X, op=ALU.max),
                          reads=[plg], writes=[mx])
                    kb.op("dve", lambda: nc.vector.tensor_scalar(out=mx[:, 0:1], in0=mx[:, 0:1], scalar1=-1.0, scalar2=None, op0=ALU.mult),
                          reads=[mx], writes=[mx])
                    kb.op("act", lambda: nc.scalar.activation(out=ex[:, :], in_=plg[:, 0:E], func=AF.Exp, bias=mx[:, 0:1]),
                          reads=[plg, mx], writes=[ex])
                    kb.op("dve", lambda: nc.vector.tensor_reduce(out=se[:, 0:1], in_=ex[:, :], axis=AX.X, op=ALU.add),
                          reads=[ex], writes=[se])
                    kb.op("dve", lambda: nc.vector.reciprocal(out=se[:, 0:1], in_=se[:, 0:1]), reads=[se], writes=[se])
                    kb.op("dve", lambda: nc.vector.tensor_scalar(out=aff[:, :], in0=ex[:, :], scalar1=se[:, 0:1], scalar2=None, op0=ALU.mult),
                          reads=[ex, se], writes=[aff])
                    kb.op("pe", lambda: nc.tensor.transpose(out=pat[0:E, 0:128], in_=aff[:, :], identity=s.ident_f[:, :]),
                          reads=[aff, s.ident_f], writes=[pat])
                    dst = affT[0] if lat else affT[1]
                    c0 = (ti if lat else ti - NTL) * 128
                    kb.op("dve", lambda: nc.vector.tensor_copy(out=dst[0:E, c0:c0 + 128], in_=pat[0:E, 0:128]), reads=[pat], writes=[dst])

                stage1(0)
                for ti in range(ntiles):
                    if ti + 1 < ntiles:
                        stage1(ti + 1)
                    stage2(ti)
                kb.barrier()
            if s.dbg:
                kb.dma("sp", [lambda e: e.dma_start(out=s.d_aff[:, :], in_=affT[0][:, :])], reads=[affT[0]])
            if getattr(s, "moe_stop", None) == "A":
                st_aff.close()
                return
            with ExitStack() as st:
                pidx = kb.ps(st, "pidx", [128, 512], F32)
                for kind in range(2 if has_ctx else 1):
                    n = T if kind == 0 else C
                    cap = CAPL if kind == 0 else CAPC
                    work = kb.sb(st, "work", [E, n], F32)
                    vals = kb.sb(st, "vals", [E, cap], F32)
                    idx = kb.sb(st, "idx", [E, cap], U32)
                    idxf = kb.sb(st, "idxf", [E, cap], F32)
                    kb.op("dve", lambda: nc.vector.tensor_copy(out=work[:, :], in_=affT[kind][:, :]), reads=[affT[kind]], writes=[work])
                    for it in range(cap // 8):
                        sl = slice(it * 8, (it + 1) * 8)
                        kb.op("dve", lambda: nc.vector.max(out=vals[:, sl], in_=work[:, :]), reads=[work], writes=[vals])
                        kb.op("dve", lambda: nc.vector.max_index(out=idx[:, sl], in_max=vals[:, sl], in_values=work[:, :]),
                              reads=[vals, work], writes=[idx])
                        kb.op("dve", lambda: nc.vector.match_replace(out=work[:, :], in_to_replace=vals[:, sl], in_values=work[:, :],
                                                                     imm_value=-1.0), reads=[vals, work], writes=[work])
                    kb.op("dve", lambda: nc.vector.tensor_copy(out=idxf[:, :], in_=idx[:, :]), reads=[idx], writes=[idxf])
                    if kind == 1:
                        kb.op("dve", lambda: nc.vector.tensor_scalar(out=idxf[:, :], in0=idxf[:, :], scalar1=float(T), scalar2=None, op0=ALU.add),
                              reads=[idxf], writes=[idxf])
                    for gi, (s0, ns, kd) in enumerate(groups):
                        if kd != kind:
                            continue
                        tf = kb.sb(st, "tf", [128, E], F32)
                        tf2 = kb.sb(st, "tf2", [128, E], F32)
                        kb.op("pe", lambda: nc.tensor.transpose(out=pidx[0:ns, 0:E], in_=idxf[0:E, s0:s0 + ns], identity=s.ident_f[0:E, 0:E]),
                              reads=[idxf, s.ident_f], writes=[pidx])
                        kb.op("dve", lambda: nc.vector.tensor_copy(out=tf[0:ns, :], in_=pidx[0:ns, 0:E]), reads=[pidx], writes=[tf])
                        kb.op("dve", lambda: nc.vector.tensor_copy(out=idxT[gi][0:ns, 0, :], in_=tf[0:ns, :]), reads=[tf], writes=[idxT[gi]])
                        for nb in range(NB):
                            kb.op("dve", lambda nb=nb: nc.vector.tensor_scalar(out=tf2[0:ns, :], in0=tf[0:ns, :], scalar1=float(NB), scalar2=float(nb),
                                                                               op0=ALU.mult, op1=ALU.add), reads=[tf], writes=[tf2])
                            kb.op("dve", lambda nb=nb: nc.vector.tensor_copy(out=idxT[gi][0:ns, nb + 1, :], in_=tf2[0:ns, :]),
                                  reads=[tf2], writes=[idxT[gi]])
                        kb.op("pe", lambda: nc.tensor.transpose(out=pidx[0:ns, 0:E], in_=vals[0:E, s0:s0 + ns], identity=s.ident_f[0:E, 0:E]),
                              reads=[vals, s.ident_f], writes=[pidx])
                        kb.op("dve", lambda: nc.vector.tensor_copy(out=gateT[gi][0:ns, :], in_=pidx[0:ns, 0:E]), reads=[pidx], writes=[gateT[gi]])
                kb.barrier()
            if s.dbg:
                kb.dma("sp", [lambda e: e.dma_start(out=s.d_idx[:, :, :], in_=idxT[0][:, :, :])], reads=[idxT[0]])
                kb.dma("sp", [lambda e: e.dma_start(out=s.d_gate[:, :], in_=gateT[0][:, :])], reads=[gateT[0]])
            st_aff.close()
            if getattr(s, "moe_stop", None) == "B":
                return
            with ExitStack() as st:
                nkind = 2 if has_ctx else 1
                g2s = [[kb.sb(st, "g2s", [128, 512], F32) for _ in range(2)] for _ in range(nkind)]
                FB = min(256, FF)
                NSUB = FB // 128
                xs = [kb.sb(st, "xs", [128, D], BF16) for _ in groups]
                xsT = [kb.sb(st, "xsT", [128, KC, S], BF16) for _ in range(2)]
                MP = 128 if has_ctx else 0
                SP = S if not has_ctx else (S - CAPC + MP)
                hidT = [kb.sb(st, "hidT", [128, FC, SP], BF16) for _ in range(2)]
                for h_ in hidT:
                    kb.op("dve", lambda h_=h_: nc.vector.memset(h_[:, :, :], 0.0), writes=[h_])
                Wg = [kb.sb(st, "Wg", [128, KC, FB], BF16) for _ in range(2)]
                Wu = [kb.sb(st, "Wu", [128, KC, FB], BF16) for _ in range(2)]
                Wd = [kb.sb(st, "Wd", [128, FC, 512], BF16) for _ in range(2)]
                sg = [kb.sb(st, "sg", [128, S], F32) for _ in range(2)]
                ys = [kb.sb(st, "ys", [128, 512], F32) for _ in range(4)]
                ptx = [kb.ps(st, "ptx", [128, 8, 128], BF16) for _ in range(2)]
                pg = [kb.ps(st, "pg", [128, 512], F32) for _ in range(2)]
                pu = [kb.ps(st, "pu", [128, 512], F32) for _ in range(2)]
                py = [kb.ps(st, "py", [128, 512], F32) for _ in range(2)]
                sc_prev = [[] for _ in range(NB)]
                sc_cur = [[] for _ in range(NB)]
                soff = []
                a = 0
                for (_, ns, _) in groups:
                    soff.append(a)
                    a += ns
                cn = {"w": 0, "px": 0, "f": 0, "d": 0, "y": 0, "py": 0}

                KS_GU = max(1, min(KC, 1024 // FB))
                KS_D = max(1, min(FC, 2))
                stg_gu = [kb.sb(st, "stg_gu", [128, KS_GU, FB], F32) for _ in range(6)]
                stg_d = [kb.sb(st, "stg_d", [128, KS_D, 512], F32) for _ in range(2)]

                def load_cast(tile, w_ap, c0, ncols, eng, ring, key, ks):
                    src = w_ap[:, c0:c0 + ncols].rearrange("(k p) n -> p k n", p=128)
                    KCn = src.shape[1]
                    for k0 in range(0, KCn, ks):
                        k1 = min(KCn, k0 + ks)
                        sg_ = ring[cn[key] % len(ring)]
                        cn[key] += 1
                        kb.dma("sp", [lambda e_: e_.dma_start(out=sg_[:, 0:k1 - k0, :], in_=src[:, k0:k1, :])], writes=[sg_])
                        if eng == "act":
                            kb.op("act", lambda: nc.scalar.copy(out=tile[:, k0:k1, 0:ncols], in_=sg_[:, 0:k1 - k0, :]), reads=[sg_], writes=[tile])
                        else:
                            kb.op("dve", lambda: nc.vector.tensor_copy(out=tile[:, k0:k1, 0:ncols], in_=sg_[:, 0:k1 - k0, :]),
                                  reads=[sg_], writes=[tile])

                cn["sgu"] = 0
                cn["sd"] = 0

                def load_gu(e, fb, slot):
                    load_cast(Wg[slot], p["wg"][e], fb * FB, FB, "act", stg_gu, "sgu", KS_GU)
                    load_cast(Wu[slot], p["wu"][e], fb * FB, FB, "dve", stg_gu, "sgu", KS_GU)

                def load_d(e, nb, slot):
                    load_cast(Wd[slot], p["wd"][e], nb * 512, 512, "act" if slot == 0 else "dve", stg_d, "sd", KS_D)

                def gather(e):
                    for gi, (s0, ns, kd) in enumerate(groups):
                        kb.dma("pool", [lambda eng, gi=gi, ns=ns: eng.indirect_dma_start(
                            out=xs[gi][0:ns, :], out_offset=None, in_=s.h2[:, :],
                            in_offset=bass.IndirectOffsetOnAxis(ap=idxT[gi][0:ns, 0, e:e + 1], axis=0))],
                            reads=[idxT[gi]], writes=[xs[gi]])

                def transposes(e):
                    xT = xsT[e % 2]
                    for gi, (s0, ns, kd) in enumerate(groups):
                        NPB = min(8, KC)
                        for k0 in range(0, KC, NPB):
                            pt = ptx[cn["px"] % 2]
                            cn["px"] += 1
                            for j in range(NPB):
                                kb.op("pe", lambda j=j: nc.tensor.transpose(out=pt[:, j, 0:ns], in_=xs[gi][0:ns, (k0 + j) * 128:(k0 + j + 1) * 128],
                                                                            identity=s.ident_b[0:ns, 0:ns]),
                                      reads=[xs[gi], s.ident_b], writes=[pt], inc=(j == NPB - 1))
                            kb.op("act", lambda: nc.scalar.copy(out=xT[:, k0:k0 + NPB, soff[gi]:soff[gi] + ns], in_=pt[:, 0:NPB, 0:ns]),
                                  reads=[pt], writes=[xT])

                def gateup(e):
                    xT = xsT[e % 2]
                    hT_ = hidT[e % 2]
                    nfb = FF // FB
                    for fb in range(nfb):
                        slot = cn["w"] % 2
                        cn["w"] += 1
                        if fb + 1 < nfb:
                            load_gu(e, fb + 1, cn["w"] % 2)
                        if fb == nfb - 1:
                            load_d(e, 0, cn["d"] % 2)
                        for sub in range(NSUB):
                            fc = fb * NSUB + sub
                            pg_, pu_, sg_ = pg[cn["f"] % 2], pu[cn["f"] % 2], sg[cn["f"] % 2]
                            cn["f"] += 1
                            for (W_, pp) in ((Wg[slot], pg_), (Wu[slot], pu_)):
                                for kc in range(KC):
                                    kb.op("pe", lambda kc=kc: nc.tensor.matmul(pp[:, 0:S], lhsT=W_[:, kc, sub * 128:(sub + 1) * 128], rhs=xT[:, kc, 0:S],
                                                                               start=(kc == 0), stop=(kc == KC - 1)),
                                          reads=[W_, xT], writes=[pp], inc=(kc == KC - 1))
                            kb.op("act", lambda: nc.scalar.activation(out=sg_[:, 0:S], in_=pg_[:, 0:S], func=AF.Silu), reads=[pg_], writes=[sg_])
                            kb.op("dve", lambda: nc.vector.tensor_tensor(out=hT_[:, fc, 0:S], in0=sg_[:, 0:S], in1=pu_[:, 0:S], op=ALU.mult),
                                  reads=[sg_, pu_], writes=[hT_])

                def down(e):
                    hT_ = hidT[e % 2]
                    for nb in range(NB):
                        Wd_ = Wd[cn["d"] % 2]
                        cn["d"] += 1
                        if nb + 1 < NB:
                            load_d(e, nb + 1, cn["d"] % 2)
                        g2 = [g2s[r][cn["d"] % 2] for r in range(nkind)]
                        for r in range(nkind):
                            s.bc_load(g2[r], s.mod_row(l, r, 5)[0:1, nb * 512:(nb + 1) * 512])
                        for gi, (s0, ns, kd) in enumerate(groups):
                            py_ = py[cn["py"] % 2]
                            cn["py"] += 1
                            ys_ = ys[cn["y"] % 4]
                            cn["y"] += 1
                            nsm = ns if kd == 0 else MP
                            for fc in range(FC):
                                kb.op("pe", lambda fc=fc: nc.tensor.matmul(py_[0:nsm, :], lhsT=hT_[:, fc, soff[gi]:soff[gi] + nsm], rhs=Wd_[:, fc, :],
                                                                           start=(fc == 0), stop=(fc == FC - 1)),
                                      reads=[hT_, Wd_], writes=[py_], inc=(fc == FC - 1))
                            kb.op("dve", lambda: nc.vector.scalar_tensor_tensor(out=ys_[0:ns, :], in0=py_[0:ns, :], scalar=gateT[gi][0:ns, e:e + 1],
                                                                                in1=g2[kd][0:ns, :], op0=ALU.mult, op1=ALU.mult),
                                  reads=[py_, gateT[gi], g2[kd]], writes=[ys_])
                            for tk in sc_prev[nb]:
                                kb._wait("pool", tk)
                            tk = kb.dma("pool", [lambda eng: eng.indirect_dma_start(
                                out=tgt, out_offset=bass.IndirectOffsetOnAxis(ap=idxT[gi][0:ns, nb + 1, e:e + 1], axis=0),
                                in_=ys_[0:ns, :], in_offset=None, compute_op=ALU.add)],
                                reads=[ys_, idxT[gi]])
                            sc_cur[nb].append(tk)
                            if gi == len(groups) - 1:
                                sc_prev[nb] = sc_cur[nb]
                                sc_cur[nb] = []

                gather(0)
                load_gu(0, 0, cn["w"] % 2)
                transposes(0)
                for e in range(E):
                    gateup(e)
                    if e + 1 < E:
                        gather(e + 1)
                        load_gu(e + 1, 0, cn["w"] % 2)
                    down(e)
                    if e + 1 < E:
                        transposes(e + 1)

def host_consts(cfg):
    T, TT = cfg.T, cfg.TT
    rows = T // cfg.GW
    row = np.repeat(np.arange(rows, dtype=np.float32), cfg.GW)
    col = np.tile(np.arange(cfg.GW, dtype=np.float32), rows)
    inv_freq = (10000.0 ** (-np.arange(0, 64, 2, dtype=np.float32) / 64)).astype(np.float32)
    ang = np.concatenate([row[:, None] * inv_freq, col[:, None] * inv_freq], axis=-1)
    cos = np.ones((128, TT), np.float32)
    sin = np.zeros((128, TT), np.float32)
    cos[:64, :T] = np.cos(ang).T
    cos[64:, :T] = np.cos(ang).T
    sin[:64, :T] = np.sin(ang).T
    sin[64:, :T] = np.sin(ang).T
    ident = np.eye(128, dtype=np.float32)
    rot = np.zeros((128, 128), np.float32)
    for i in range(64):
        rot[i + 64, i] = -1.0
        rot[i, i + 64] = 1.0
    return {"k_rope_c": cos, "k_rope_s": sin, "k_ident": ident, "k_rot": rot}


def make_in_maps(cfg, inputs, n_cores):
    consts = host_consts(cfg)
    shared = dict(consts)
    shared["c_ctx"] = np.ascontiguousarray(inputs["c_ctx"]).reshape(1, -1)
    for k, v in inputs.items():
        if k in ("x", "c", "ctx", "c_ctx"):
            continue
        a = np.ascontiguousarray(v)
        if a.ndim == 1:
            a = a.reshape(1, -1)
        shared[k] = a
    maps = []
    for b in range(n_cores):
        m = dict(shared)
        m["x"] = np.ascontiguousarray(inputs["x"][b])
        m["c"] = np.ascontiguousarray(inputs["c"][b]).reshape(1, -1)
        m["ctx"] = np.ascontiguousarray(inputs["ctx"][b])
        maps.append(m)
    return maps


def kernel(**inputs):
    cfg = Cfg()
    prog = Prog(cfg)
    n = inputs["x"].shape[0]
    maps = make_in_maps(cfg, inputs, n)
    res = run_bass_kernel_spmd(prog.nc, maps, core_ids=list(range(n)))
    return np.stack([np.asarray(r["out"]) for r in res.results], axis=0).astype(np.float32)
```
